# Optimizing a Trainium2 kernel written in Bass

```python
import math
import jax
import jax.numpy as jnp
from jax import lax
import numpy as np

D_MODEL = 1024
BATCH = 16
SEQ = 4096
DEPTH = 4
DEC_BATCH = 2
DEC_SEQ = 8192
PAST_LEN = 128

N_HEADS = 4
HEAD_DIM = 128
MIX_W = N_HEADS * HEAD_DIM
N_BRANCH = 4
RG_CONV = 4
SHORT_CONV = 3
LRU_C = 8.0
MLSTM_CHUNK = 128
HGRN_CHUNK = 32
HYENA_EMB = 33
HYENA_BANDS = (HYENA_EMB - 1) // 2
HYENA_HID = 64
HYENA_FAST = 0.3
HYENA_SLOW = 1.5
HYENA_TARGET = 1e-2
D_FF = 2 * D_MODEL
FFN_CONV = 3
D_PLE = 256
EPS = 1e-6
STAB_INIT = -1e30

SEG_A = 2 * MIX_W
SEG_B = 4 * MIX_W + 4 * N_HEADS
SEG_C = 3 * MIX_W
SEG_D = 5 * MIX_W
SEG_G = N_BRANCH * D_MODEL
OFF_A = 0
OFF_B = OFF_A + SEG_A
OFF_C = OFF_B + SEG_B
OFF_D = OFF_C + SEG_C
OFF_G = OFF_D + SEG_D
D_IN = OFF_G + SEG_G

SHARED_PARAMS = ('hgrn_lb_logits', 'final_norm')

kernel_name = 'hybrid_bidir_encoder_rglru_mlstm_hyena_hgrn2'


def _rmsnorm(x, g):
    xf = x.astype(jnp.float32)
    y = xf * lax.rsqrt(jnp.mean(xf * xf, axis=-1, keepdims=True) + EPS)
    return (y * g.astype(jnp.float32)).astype(x.dtype)


def _head_rmsnorm(x, g):
    shp = x.shape
    xh = x.reshape(shp[:-1] + (N_HEADS, HEAD_DIM))
    xh = xh * lax.rsqrt(jnp.mean(xh * xh, axis=-1, keepdims=True) + EPS)
    return xh.reshape(shp) * g.astype(jnp.float32)


def _proj(xn, w_in, b_in, off, width):
    return xn @ w_in[:, off:off + width] + b_in[off:off + width]


def _dwconv(x, w, b):
    kw = w.shape[0]
    s = x.shape[1]
    lo = (kw - 1) // 2
    xp = jnp.pad(x, ((0, 0), (lo, kw - 1 - lo), (0, 0)))
    y = b + w[0] * xp[:, 0:s]
    for j in range(1, kw):
        y = y + w[j] * xp[:, j:j + s]
    return y


def _linear_scan(a, u, reverse):
    def combine(left, right):
        a_l, u_l = left
        a_r, u_r = right
        return a_l * a_r, a_r * u_l + u_r
    _, h = lax.associative_scan(combine, (a, u), reverse=reverse, axis=1)
    return h


def _rglru_branch(xn, w_in, b_in, conv_w, conv_b, gate_w, gate_b, lam):
    f32 = jnp.float32
    bsz, s, _ = xn.shape
    x = _dwconv(_proj(xn, w_in, b_in, OFF_A, MIX_W), conv_w, conv_b).astype(f32)
    y = jax.nn.gelu(_proj(xn, w_in, b_in, OFF_A + MIX_W, MIX_W).astype(f32))
    xh = x.reshape(bsz, s, N_HEADS, HEAD_DIM)
    hs = []
    for d in range(2):
        g = jnp.einsum('bshi,ghij->gbshj', xh, gate_w[d].astype(f32)).reshape(2, bsz, s, MIX_W)
        g = jax.nn.sigmoid(g + gate_b[d][:, None, None, :].astype(f32))
        log_a = -LRU_C * g[0] * jax.nn.softplus(-lam[d].astype(f32))
        u = jnp.sqrt(-jnp.expm1(2.0 * log_a)) * (g[1] * x)
        hs.append(_linear_scan(jnp.exp(log_a), u, reverse=(d == 1)))
    return ((hs[0] + hs[1]) * y).astype(xn.dtype)


def _to_chunks(t, cl):
    bsz, s, _ = t.shape
    return t.reshape(bsz, s // cl, cl, N_HEADS, -1).transpose(0, 3, 1, 2, 4)


def _gate_chunks(t, cl):
    bsz, s, _ = t.shape
    return t.reshape(bsz, s // cl, cl, N_HEADS).transpose(0, 3, 1, 2)


def _from_chunks(t):
    bsz, nh, nc, cl, d = t.shape
    return t.transpose(0, 2, 3, 1, 4).reshape(bsz, nc * cl, nh * d)


def _mlstm_chunkwise(q, k, v, ig, lf):
    bsz, nh, nc, cl, dk = q.shape
    dv = v.shape[-1]
    b = jnp.cumsum(lf, axis=-1)
    b_last = b[..., -1]
    lw = b_last[..., None] - b + ig
    m_loc = jnp.max(lw, axis=-1)
    w = jnp.exp(lw - m_loc[..., None])
    c_loc = jnp.einsum('bhcs,bhcsk,bhcsv->bhckv', w, k, v)
    n_loc = jnp.einsum('bhcs,bhcsk->bhck', w, k)

    def step(carry, xs):
        c_st, n_st, m_st = carry
        c_l, n_l, m_l, bl = xs
        m_new = jnp.maximum(bl + m_st, m_l)
        s_old = jnp.exp(bl + m_st - m_new)
        s_new = jnp.exp(m_l - m_new)
        c_next = s_old[..., None, None] * c_st + s_new[..., None, None] * c_l
        n_next = s_old[..., None] * n_st + s_new[..., None] * n_l
        return (c_next, n_next, m_new), (c_st, n_st, m_st)

    init = (jnp.zeros((bsz, nh, dk, dv), q.dtype), jnp.zeros((bsz, nh, dk), q.dtype),
            jnp.full((bsz, nh), STAB_INIT, q.dtype))
    xs = (jnp.moveaxis(c_loc, 2, 0), jnp.moveaxis(n_loc, 2, 0), jnp.moveaxis(m_loc, 2, 0), jnp.moveaxis(b_last, 2, 0))
    _, (c_prev, n_prev, m_prev) = lax.scan(step, init, xs)
    c_prev = jnp.moveaxis(c_prev, 0, 2)
    n_prev = jnp.moveaxis(n_prev, 0, 2)
    m_prev = jnp.moveaxis(m_prev, 0, 2)

    lower = jnp.tril(jnp.ones((cl, cl), dtype=bool))
    d_mat = jnp.where(lower, b[..., :, None] - b[..., None, :] + ig[..., None, :], -jnp.inf)
    m_inter = b + m_prev[..., None]
    m_all = jnp.maximum(jnp.max(d_mat, axis=-1), m_inter)
    a_mat = jnp.exp(d_mat - m_all[..., None]) * jnp.einsum('bhcjd,bhcsd->bhcjs', q, k)
    s_inter = jnp.exp(m_inter - m_all)
    num = (jnp.einsum('bhcjs,bhcsv->bhcjv', a_mat, v)
           + s_inter[..., None] * jnp.einsum('bhcjd,bhcdv->bhcjv', q, c_prev))
    den = jnp.sum(a_mat, axis=-1) + s_inter * jnp.einsum('bhcjd,bhcd->bhcj', q, n_prev)
    return num / jnp.maximum(jnp.abs(den), jnp.exp(-m_all))[..., None]


def _mlstm_branch(xn, w_in, b_in, norm_g):
    f32 = jnp.float32
    bsz, s, _ = xn.shape
    q = _proj(xn, w_in, b_in, OFF_B, MIX_W).astype(f32)
    k = _proj(xn, w_in, b_in, OFF_B + MIX_W, MIX_W).astype(f32) * (HEAD_DIM ** -0.5)
    v = _proj(xn, w_in, b_in, OFF_B + 2 * MIX_W, MIX_W).astype(f32)
    o = _proj(xn, w_in, b_in, OFF_B + 3 * MIX_W, MIX_W).astype(f32)
    g = _proj(xn, w_in, b_in, OFF_B + 4 * MIX_W, 4 * N_HEADS).astype(f32).reshape(bsz, s, 4, N_HEADS)

    def run(qd, kd, vd, gi, gf):
        cl = MLSTM_CHUNK
        h = _mlstm_chunkwise(_to_chunks(qd, cl), _to_chunks(kd, cl), _to_chunks(vd, cl),
                             _gate_chunks(gi, cl), _gate_chunks(jax.nn.log_sigmoid(gf), cl))
        return _from_chunks(h)

    def rv(t):
        return jnp.flip(t, axis=1)

    h_fwd = run(q, k, v, g[:, :, 0], g[:, :, 1])
    h_bwd = rv(run(rv(q), rv(k), rv(v), rv(g[:, :, 2]), rv(g[:, :, 3])))
    return (jax.nn.sigmoid(o) * _head_rmsnorm(h_fwd + h_bwd, norm_g)).astype(xn.dtype)


def _hyena_filters(seq_len, w1, b1, fr1, w2, b2, fr2, w3):
    f32 = jnp.float32
    t = jnp.linspace(0.0, 1.0, seq_len, dtype=f32)[:, None]
    omega = (2.0 * math.pi / seq_len) * jnp.arange(seq_len, dtype=f32)[:, None]
    bands = jnp.linspace(1e-4, HYENA_BANDS - 1, HYENA_BANDS, dtype=f32)[None, :]
    feat = jnp.concatenate([t, jnp.cos(omega * bands), -jnp.sin(omega * bands)], axis=-1)
    hid = jnp.sin(fr1.astype(f32) * (feat @ w1.astype(f32) + b1.astype(f32)))
    hid = jnp.sin(fr2.astype(f32) * (hid @ w2.astype(f32) + b2.astype(f32)))
    filt = (hid @ w3.astype(f32)).reshape(seq_len, 2, MIX_W)
    rate = jnp.abs(jnp.linspace(math.log(HYENA_TARGET) / HYENA_FAST, math.log(HYENA_TARGET) / HYENA_SLOW,
                                MIX_W, dtype=f32))
    return filt * jnp.exp(-t * rate)[:, None, :]


def _fft_conv(z, h):
    seq_len = z.shape[1]
    n = 2 * seq_len
    zf = jnp.fft.rfft(z, n=n, axis=1)
    hf = jnp.fft.rfft(h, n=n, axis=0)
    return jnp.fft.irfft(zf * hf[None], n=n, axis=1)[:, :seq_len]


def _hyena_branch(xn, w_in, b_in, conv_w, conv_b, w1, b1, fr1, w2, b2, fr2, w3, skip):
    f32 = jnp.float32
    u = _dwconv(_proj(xn, w_in, b_in, OFF_C, SEG_C), conv_w, conv_b).astype(f32)
    v = u[..., :MIX_W]
    x1 = u[..., MIX_W:2 * MIX_W]
    x2 = u[..., 2 * MIX_W:]
    filt = _hyena_filters(xn.shape[1], w1, b1, fr1, w2, b2, fr2, w3)
    z = x2 * v
    z = (_fft_conv(z, filt[:, 0]) + jnp.flip(_fft_conv(jnp.flip(z, axis=1), filt[:, 1]), axis=1)
         + skip.astype(f32) * z)
    return (x1 * z).astype(xn.dtype)


def _hgrn_lower_bounds(lb_logits):
    p = jax.nn.softmax(lb_logits.astype(jnp.float32), axis=0)
    c = jnp.cumsum(p, axis=0)
    return c - c[0:1]


def _to_scan_chunks(t, cl):
    bsz, s, _ = t.shape
    return t.reshape(bsz, s // cl, cl, N_HEADS, -1).transpose(1, 0, 3, 2, 4)


def _from_scan_chunks(t):
    nc, bsz, nh, cl, d = t.shape
    return t.transpose(1, 0, 3, 2, 4).reshape(bsz, nc * cl, nh * d)


def _hgrn2_chunkwise(q, k, lf, v):
    cl = q.shape[3]
    lower = jnp.tril(jnp.ones((cl, cl), dtype=bool))[:, :, None]

    def step(state, xs):
        qc, kc, lfc, vc = xs
        b = jnp.cumsum(lfc, axis=2)
        o_inter = jnp.einsum('bhjk,bhkv->bhjv', qc * jnp.exp(b), state)
        dec = jnp.exp(jnp.where(lower, b[:, :, :, None, :] - b[:, :, None, :, :], -jnp.inf))
        a_mat = jnp.einsum('bhjk,bhsk,bhjsk->bhjs', qc, kc, dec)
        o_intra = jnp.einsum('bhjs,bhsv->bhjv', a_mat, vc)
        b_last = b[:, :, -1:, :]
        state_new = (jnp.exp(b_last[:, :, 0, :])[..., None] * state
                     + jnp.einsum('bhsk,bhsv->bhkv', kc * jnp.exp(b_last - b), vc))
        return state_new, o_inter + o_intra

    nc, bsz, nh, _, dk = q.shape
    init = jnp.zeros((bsz, nh, dk, v.shape[-1]), q.dtype)
    _, o = lax.scan(step, init, (q, k, lf, v))
    return o


def _hgrn2_branch(xn, w_in, b_in, lb, norm_g):
    f32 = jnp.float32
    q = jax.nn.silu(_proj(xn, w_in, b_in, OFF_D, MIX_W).astype(f32))
    f_fwd = _proj(xn, w_in, b_in, OFF_D + MIX_W, MIX_W).astype(f32)
    f_bwd = _proj(xn, w_in, b_in, OFF_D + 2 * MIX_W, MIX_W).astype(f32)
    i = _proj(xn, w_in, b_in, OFF_D + 3 * MIX_W, MIX_W).astype(f32)
    g = _proj(xn, w_in, b_in, OFF_D + 4 * MIX_W, MIX_W).astype(f32)
    log_lb = jnp.log(lb)
    log_1mlb = jnp.log1p(-lb)

    def run(qd, fd, idd):
        lf = jnp.logaddexp(log_lb, log_1mlb + jax.nn.log_sigmoid(fd))
        kd = (1.0 - lb) * jax.nn.sigmoid(-fd)
        cl = HGRN_CHUNK
        o = _hgrn2_chunkwise(_to_scan_chunks(qd, cl), _to_scan_chunks(kd, cl),
                             _to_scan_chunks(lf, cl), _to_scan_chunks(idd, cl))
        return _from_scan_chunks(o)

    def rv(t):
        return jnp.flip(t, axis=1)

    o_fwd = run(q, f_fwd, i)
    o_bwd = rv(run(rv(q), rv(f_bwd), rv(i)))
    return (jax.nn.sigmoid(g) * _head_rmsnorm(o_fwd + o_bwd, norm_g)).astype(xn.dtype)


def _layer(h, p_i, lp, lb):
    xn = _rmsnorm(h, lp['norm_mix'])
    w_in = lp['w_in']
    b_in = lp['b_in']
    branches = (
        _rglru_branch(xn, w_in, b_in, lp['conv_a_w'], lp['conv_a_b'], lp['rglru_w'], lp['rglru_b'], lp['rglru_lam']),
        _mlstm_branch(xn, w_in, b_in, lp['mlstm_norm']),
        _hyena_branch(xn, w_in, b_in, lp['hyena_conv_w'], lp['hyena_conv_b'], lp['hyena_w1'], lp['hyena_b1'],
                      lp['hyena_freq1'], lp['hyena_w2'], lp['hyena_b2'], lp['hyena_freq2'], lp['hyena_w3'],
                      lp['hyena_skip']),
        _hgrn2_branch(xn, w_in, b_in, lb, lp['hgrn_norm']),
    )
    merged = None
    for idx in range(N_BRANCH):
        gate = jax.nn.sigmoid(_proj(xn, w_in, b_in, OFF_G + idx * D_MODEL, D_MODEL))
        term = gate * (branches[idx] @ lp['w_branch'][idx])
        merged = term if merged is None else merged + term
    h = h + merged @ lp['w_out']
    xn = _rmsnorm(h, lp['norm_ffn'])
    u = _dwconv(xn @ lp['w_up'], lp['ffn_conv_w'], lp['ffn_conv_b'])
    h = h + (jax.nn.gelu(u[..., :D_FF]) * u[..., D_FF:]) @ lp['w_down']
    gate = jax.nn.sigmoid(_rmsnorm(h, lp['norm_ple']) @ lp['w_ple_gate'])
    h = h + gate * (p_i @ lp['w_ple'])
    return h


def _trunk(x, p, params):
    lb_all = _hgrn_lower_bounds(params['hgrn_lb_logits'])
    h = x
    for i in range(DEPTH):
        lp = {name: arr[i] for name, arr in params.items() if name not in SHARED_PARAMS}
        h = _layer(h, p[i], lp, lb_all[i])
    return _rmsnorm(h, params['final_norm'])


def setup_inputs(seed: int = 0) -> dict:
    key = jax.random.key(seed)
    ks = jax.random.split(key, 40)
    f32 = jnp.float32

    def nrm(i, shape, scale):
        return jax.random.normal(ks[i], shape, f32) * scale

    def gain(i, shape):
        return 1.0 + nrm(i, shape, 0.01)

    b_in = nrm(6, (DEPTH, D_IN), 0.01)
    f_cols = np.array([OFF_B + 4 * MIX_W + N_HEADS + j for j in range(N_HEADS)]
                      + [OFF_B + 4 * MIX_W + 3 * N_HEADS + j for j in range(N_HEADS)])
    b_in = b_in.at[:, f_cols].add(jnp.tile(jnp.linspace(3.0, 6.0, N_HEADS, dtype=f32), 2))
    a_c = jax.random.uniform(ks[11], (DEPTH, 2, MIX_W), f32, minval=0.9, maxval=0.999)
    a_base = a_c ** (1.0 / LRU_C)
    rglru_lam = jnp.log(a_base) - jnp.log1p(-a_base)
    return {
        'x_prompt': nrm(0, (BATCH, SEQ, D_MODEL), 1.0),
        'x_sample': nrm(1, (DEC_BATCH, DEC_SEQ, D_MODEL), 1.0),
        'p_prompt': nrm(2, (DEPTH, BATCH, SEQ, D_PLE), 1.0),
        'p_sample': nrm(3, (DEPTH, DEC_BATCH, DEC_SEQ, D_PLE), 1.0),
        'norm_mix': gain(4, (DEPTH, D_MODEL)),
        'w_in': nrm(5, (DEPTH, D_MODEL, D_IN), D_MODEL ** -0.5),
        'b_in': b_in,
        'conv_a_w': nrm(7, (DEPTH, RG_CONV, MIX_W), RG_CONV ** -0.5),
        'conv_a_b': nrm(8, (DEPTH, MIX_W), 0.01),
        'rglru_w': nrm(9, (DEPTH, 2, 2, N_HEADS, HEAD_DIM, HEAD_DIM), HEAD_DIM ** -0.5),
        'rglru_b': nrm(10, (DEPTH, 2, 2, MIX_W), 0.01),
        'rglru_lam': rglru_lam,
        'mlstm_norm': gain(12, (DEPTH, MIX_W)),
        'hyena_conv_w': nrm(13, (DEPTH, SHORT_CONV, SEG_C), SHORT_CONV ** -0.5),
        'hyena_conv_b': nrm(14, (DEPTH, SEG_C), 0.01),
        'hyena_w1': nrm(15, (DEPTH, HYENA_EMB, HYENA_HID), HYENA_EMB ** -0.5),
        'hyena_b1': nrm(16, (DEPTH, HYENA_HID), 0.01),
        'hyena_freq1': 1.0 + nrm(17, (DEPTH, HYENA_HID), 0.1),
        'hyena_w2': nrm(18, (DEPTH, HYENA_HID, HYENA_HID), HYENA_HID ** -0.5),
        'hyena_b2': nrm(19, (DEPTH, HYENA_HID), 0.01),
        'hyena_freq2': 1.0 + nrm(20, (DEPTH, HYENA_HID), 0.1),
        'hyena_w3': nrm(21, (DEPTH, HYENA_HID, 2 * MIX_W), 0.04 * HYENA_HID ** -0.5),
        'hyena_skip': nrm(22, (DEPTH, MIX_W), 1.0),
        'hgrn_lb_logits': nrm(23, (DEPTH, MIX_W), 0.1),
        'hgrn_norm': gain(24, (DEPTH, MIX_W)),
        'w_branch': nrm(25, (DEPTH, N_BRANCH, MIX_W, D_MODEL), MIX_W ** -0.5),
        'w_out': nrm(26, (DEPTH, D_MODEL, D_MODEL), D_MODEL ** -0.5),
        'norm_ffn': gain(27, (DEPTH, D_MODEL)),
        'w_up': nrm(28, (DEPTH, D_MODEL, 2 * D_FF), D_MODEL ** -0.5),
        'ffn_conv_w': nrm(29, (DEPTH, FFN_CONV, 2 * D_FF), FFN_CONV ** -0.5),
        'ffn_conv_b': nrm(30, (DEPTH, 2 * D_FF), 0.01),
        'w_down': nrm(31, (DEPTH, D_FF, D_MODEL), D_FF ** -0.5),
        'norm_ple': gain(32, (DEPTH, D_MODEL)),
        'w_ple_gate': nrm(33, (DEPTH, D_MODEL, D_MODEL), D_MODEL ** -0.5),
        'w_ple': nrm(34, (DEPTH, D_PLE, D_MODEL), D_PLE ** -0.5),
        'final_norm': gain(35, (D_MODEL,)),
    }


def reference(x_prompt, x_sample, p_prompt, p_sample, norm_mix, w_in, b_in, conv_a_w, conv_a_b, rglru_w,
              rglru_b, rglru_lam, mlstm_norm, hyena_conv_w, hyena_conv_b, hyena_w1, hyena_b1, hyena_freq1,
              hyena_w2, hyena_b2, hyena_freq2, hyena_w3, hyena_skip, hgrn_lb_logits, hgrn_norm, w_branch,
              w_out, norm_ffn, w_up, ffn_conv_w, ffn_conv_b, w_down, norm_ple, w_ple_gate, w_ple, final_norm):
    params = {
        'norm_mix': norm_mix, 'w_in': w_in, 'b_in': b_in,
        'conv_a_w': conv_a_w, 'conv_a_b': conv_a_b,
        'rglru_w': rglru_w, 'rglru_b': rglru_b, 'rglru_lam': rglru_lam,
        'mlstm_norm': mlstm_norm,
        'hyena_conv_w': hyena_conv_w, 'hyena_conv_b': hyena_conv_b,
        'hyena_w1': hyena_w1, 'hyena_b1': hyena_b1, 'hyena_freq1': hyena_freq1,
        'hyena_w2': hyena_w2, 'hyena_b2': hyena_b2, 'hyena_freq2': hyena_freq2,
        'hyena_w3': hyena_w3, 'hyena_skip': hyena_skip,
        'hgrn_lb_logits': hgrn_lb_logits, 'hgrn_norm': hgrn_norm,
        'w_branch': w_branch, 'w_out': w_out,
        'norm_ffn': norm_ffn, 'w_up': w_up, 'ffn_conv_w': ffn_conv_w, 'ffn_conv_b': ffn_conv_b, 'w_down': w_down,
        'norm_ple': norm_ple, 'w_ple_gate': w_ple_gate, 'w_ple': w_ple,
        'final_norm': final_norm,
    }
    y_prompt = _trunk(x_prompt, p_prompt, params)
    y_sample = _trunk(x_sample, p_sample, params)
    return (y_prompt, y_sample)
```

```python
import math
import os
from contextlib import ExitStack
import numpy as np
import ml_dtypes
import concourse.bass as bass
import concourse.mybir as mybir
from concourse.bass_utils import run_bass_kernel_spmd

F32, BF16 = mybir.dt.float32, mybir.dt.bfloat16
I32 = mybir.dt.int32
PI_S = 3.1415925
AF = mybir.ActivationFunctionType
ALU = mybir.AluOpType
AX = mybir.AxisListType

D = 1024
MW = 512
DPLE = 256
DFF = 2048
OFF_A, OFF_B, OFF_C, OFF_D = 0, 1024, 1024 + 2064, 1024 + 2064 + 1536
OFF_G = OFF_D + 2560
DIN = OFF_G + 4096
EPS = 1e-6
NS = 16


class FW:
    def __init__(self, nc):
        self.nc = nc
        self.es = ExitStack()
        self.eng = {'pe': nc.tensor, 'act': nc.scalar, 'dve': nc.vector, 'pool': nc.gpsimd, 'sp': nc.sync}
        self.sem = {e: self.es.enter_context(nc.semaphore("pg_" + e)) for e in ('pe', 'act', 'dve', 'pool')}
        self.cnt = {e: 0 for e in self.sem}
        self.dsem = {q: [self.es.enter_context(nc.semaphore("dq_%s_%d" % (q, i))) for i in range(NS)]
                     for q in ('sp', 'pool', 'act')}
        self.dcnt = {q: 0 for q in self.dsem}
        self.known = {e: {} for e in self.eng}
        self.semobj = {}
        for e, s in self.sem.items():
            self.semobj[('c', e)] = s
        for q, l in self.dsem.items():
            for i, s in enumerate(l):
                self.semobj[('d', q, i)] = s
        self.lastw = {}
        self.readers = {}
        self.dram = set()
        self.psum = [self.es.enter_context(nc.psum_tensor("psb%d" % i, [128, 512], F32))[:] for i in range(8)]
        self.ninst = 0

    def _key(self, ap):
        return ap if isinstance(ap, str) else ap.name

    def _wait(self, e, events):
        kn = self.known[e]
        for (sk, v) in events:
            if sk == ('c', e) and e == 'pe':
                continue
            if kn.get(sk, 0) >= v:
                continue
            self.eng[e].wait_ge(self.semobj[sk], v)
            kn[sk] = v

    def _deps(self, r, w):
        ev = []
        for a in r:
            k = self._key(a)
            if k in self.dram:
                continue
            if k in self.lastw:
                ev.append(self.lastw[k])
        for a in w:
            k = self._key(a)
            if k in self.dram:
                continue
            if k in self.lastw:
                ev.append(self.lastw[k])
            ev.extend(self.readers.get(k, ()))
        return ev

    def _record(self, ev, r, w):
        for a in r:
            k = self._key(a)
            if k in self.dram:
                continue
            self.readers.setdefault(k, []).append(ev)
        for a in w:
            k = self._key(a)
            if k in self.dram:
                continue
            self.lastw[k] = ev
            self.readers[k] = []

    def op(self, e, fn, r=(), w=()):
        self._wait(e, self._deps(r, w))
        ins = fn(self.eng[e])
        self.cnt[e] += 1
        ins.then_inc(self.sem[e], 1)
        self._record((('c', e), self.cnt[e]), r, w)
        self.ninst += 1

    def dma(self, q, out, in_, **kw):
        j = self.dcnt[q]
        self.dcnt[q] += 1
        si, rnd = j % NS, j // NS
        ev = list(self._deps([in_], [out]))
        if rnd > 0:
            ev.append((('d', q, si), 16 * rnd))
        self._wait(q, ev)
        self.eng[q].dma_start(out=out, in_=in_, **kw).then_inc(self.dsem[q][si], 16)
        self._record((('d', q, si), 16 * (rnd + 1)), [in_], [out])
        self.ninst += 1

    def barrier(self):
        evs = [(('c', e), self.cnt[e]) for e in self.cnt if self.cnt[e] > 0]
        for q in self.dsem:
            n = self.dcnt[q]
            for si in range(NS):
                cntq = (n - si + NS - 1) // NS if n > si else 0
                if cntq > 0:
                    evs.append((('d', q, si), 16 * cntq))
        for e in self.eng:
            self._wait(e, evs)
        self.lastw = {}
        self.readers = {}

    def mm(self, out, lhsT, rhs, start=True, stop=True):
        self.op('pe', lambda e: e.matmul(out, lhsT, rhs, start=start, stop=stop), r=[lhsT, rhs], w=[out])

    def act(self, out, in_, func, bias=None, scale=None, accum=None, extra_r=()):
        kw = {}
        r = [in_] + list(extra_r)
        if bias is not None:
            kw['bias'] = bias
            if not isinstance(bias, (int, float)):
                r.append(bias)
        if scale is not None:
            kw['scale'] = scale
            if not isinstance(scale, (int, float)):
                r.append(scale)
        w = [out]
        if accum is not None:
            kw['accum_out'] = accum
            w.append(accum)
        self.op('act', lambda e: e.activation(out=out, in_=in_, func=func, **kw), r=r, w=w)

    def tt(self, out, a, b, op, eng='dve'):
        self.op(eng, lambda e: e.tensor_tensor(out=out, in0=a, in1=b, op=op), r=[a, b], w=[out])

    def ts(self, out, a, s1, s2=None, op0=ALU.mult, op1=None, eng='dve'):
        r = [a] + [s for s in (s1, s2) if s is not None and not isinstance(s, (int, float))]
        if op1 is None:
            self.op(eng, lambda e: e.tensor_scalar(out=out, in0=a, scalar1=s1, scalar2=None, op0=op0), r=r, w=[out])
        else:
            self.op(eng, lambda e: e.tensor_scalar(out=out, in0=a, scalar1=s1, scalar2=s2, op0=op0, op1=op1),
                    r=r, w=[out])

    def stt(self, out, a, s, b, op0, op1, eng='dve'):
        r = [a, b] + ([] if isinstance(s, (int, float)) else [s])
        self.op(eng, lambda e: e.scalar_tensor_tensor(out=out, in0=a, scalar=s, in1=b, op0=op0, op1=op1),
                r=r, w=[out])

    def copy(self, out, in_, eng='dve'):
        if eng == 'act':
            self.op('act', lambda e: e.copy(out=out, in_=in_), r=[in_], w=[out])
        else:
            self.op(eng, lambda e: e.tensor_copy(out=out, in_=in_), r=[in_], w=[out])

    def memset(self, ap, v, eng='dve'):
        self.op(eng, lambda e: e.memset(ap, v), r=[], w=[ap])

    def scan(self, out, d0, d1, init, op0=ALU.mult, op1=ALU.add):
        r = [d0, d1] + ([] if isinstance(init, (int, float)) else [init])
        self.op('dve', lambda e: e.tensor_tensor_scan(out=out, data0=d0, data1=d1, initial=init, op0=op0, op1=op1),
                r=r, w=[out])

    def tss(self, out, in_, scalar, op, eng='dve'):
        self.op(eng, lambda e: e.tensor_single_scalar(out=out, in_=in_, scalar=scalar, op=op), r=[in_], w=[out])

    def rsum(self, out, in_):
        self.op('dve', lambda e: e.reduce_sum(out=out, in_=in_, axis=AX.X), r=[in_], w=[out])


PHASE_LOG = []


class Phase:
    def __init__(self, fw, tag):
        self.fw, self.tag, self.es, self.n = fw, tag, ExitStack(), 0

    def __enter__(self):
        return self

    def sb(self, shape, dt=F32, name=None):
        self.n += 1
        nm = "%s_%s%d" % (self.tag, name or "t", self.n)
        return self.es.enter_context(self.fw.nc.sbuf_tensor(nm, list(shape), dt))[:]

    def __exit__(self, *a):
        self.fw.barrier()
        self.es.close()
        PHASE_LOG.append((self.tag, self.fw.cnt['pe'], self.fw.cnt['act'], self.fw.cnt['dve']))
        return False


def build(seqs, depth, dbg=(), feat=('A', 'B', 'C', 'D', 'ffn', 'ple')):
    T = sum(seqs)
    soff = [sum(seqs[:i]) for i in range(len(seqs))]
    assert all(L % 512 == 0 for L in seqs)
    Ls = sorted(set(seqs))
    nc = bass.Bass("TRN2", target_bir_lowering=False)
    fw = FW(nc)
    outs = {}

    def dt_in(name, shape, dt=F32):
        a = nc.dram_tensor(name, list(shape), dt, kind="ExternalInput").ap()
        fw.dram.add(a.name)
        return a

    def dt_scr(name, shape, dt=F32):
        kind = "ExternalOutput" if name in dbg else "Internal"
        a = nc.dram_tensor(name, list(shape), dt, kind=kind).ap()
        fw.dram.add(a.name)
        return a

    xtok = dt_in("xtok", [T, D])
    ptok = dt_in("ptok", [depth, T, DPLE])
    W = {}
    wshapes = dict(
        norm_mix=[depth, D], w_in=[depth, D, DIN], b_in=[depth, DIN], conv_a_w=[depth, 4, MW], conv_a_b=[depth, MW],
        rglru_w=[depth, 2, 2, 4, 128, 128], rglru_b=[depth, 2, 2, MW], rglru_lam=[depth, 2, MW],
        mlstm_norm=[depth, MW], hyena_conv_w=[depth, 3, 1536], hyena_conv_b=[depth, 1536],
        hyena_w1=[depth, 33, 64], hyena_b1=[depth, 64], hyena_freq1=[depth, 64], hyena_w2=[depth, 64, 64],
        hyena_b2=[depth, 64], hyena_freq2=[depth, 64], hyena_w3=[depth, 64, 1024], hyena_skip=[depth, MW],
        hgrn_lb_logits=[4, MW], hgrn_norm=[depth, MW], w_branch=[depth, 4, MW, D], w_out=[depth, D, D],
        norm_ffn=[depth, D], w_up=[depth, D, 2 * DFF], ffn_conv_w=[depth, 3, 2 * DFF], ffn_conv_b=[depth, 2 * DFF],
        w_down=[depth, DFF, D], norm_ple=[depth, D], w_ple_gate=[depth, D, D], w_ple=[depth, DPLE, D],
        final_norm=[D])
    for k, s in wshapes.items():
        W[k] = dt_in(k, s)
    c_ident = dt_in("c_ident", [128, 128])
    c_jrev = dt_in("c_jrev", [128, 128])
    c_mask = dt_in("c_mask", [128, 128])
    c_rst = dt_in("c_rst", [128, 512])
    c_cmask = dt_in("c_cmask", [128, 512])
    c_rmask = dt_in("c_rmask", [128, 4])
    c_rst128 = dt_in("c_rst128", [128, 512])
    c_sel = dt_in("c_sel", [8, 1024])
    c_mask128 = dt_in("c_mask128", [128, 128])
    c_r0 = dt_in("c_r0", [128, 128])
    c_t0 = dt_in("c_t0", [128, 512])
    c_rate = dt_in("c_rate", [512])
    c_feat = [dt_in("c_feat%d" % i, [33, L]) for i, L in enumerate(Ls)]
    c_nt = [dt_in("c_nt%d" % i, [L]) for i, L in enumerate(Ls)]
    c_fx = [dt_in("c_fx%d" % i, [2, 128, L // 128]) for i, L in enumerate(Ls)]
    c_ix = [dt_in("c_ix%d" % i, [2, 128, L // 128]) for i, L in enumerate(Ls)]
    ytok = nc.dram_tensor("ytok", [T, D], F32, kind="ExternalOutput").ap()
    fw.dram.add(ytok.name)

    H = [dt_scr("H0", [T, D]), dt_scr("H1", [T, D])]
    XN = dt_scr("XN", [D, T], BF16)
    XNR = dt_scr("XNR", [D, T], BF16)
    PA = dt_scr("PA", [1024, T])
    HF = dt_scr("HF", [MW, T])
    BR = [dt_scr("BR%d" % i, [MW, T], BF16) for i in range(4)]
    MG = dt_scr("MG", [D, T], BF16)
    QT = dt_scr("QT", [2, 4, 128, T], BF16)
    KT = dt_scr("KT", [2, 4, 128, T], BF16)
    KH = dt_scr("KH", [2, 4, T, 128], BF16)
    VV = dt_scr("VV", [2, T, 4, 132], BF16)
    DL = dt_scr("DL", [2, 4, 128, T // 32])
    OO = dt_scr("OO", [2, T, 4, 132])
    OG = dt_scr("OG", [T, MW])
    if 'C' in feat:
        TBFC = [dt_scr("TBFC%d" % i, [L // 128, 128, L // 128, 128], BF16) for i, L in enumerate(Ls)]
        TBFS = [dt_scr("TBFS%d" % i, [L // 128, 128, L // 128, 128], BF16) for i, L in enumerate(Ls)]
        TBIC = [dt_scr("TBIC%d" % i, [L // 512, 128, L // 128, 512], BF16) for i, L in enumerate(Ls)]
        TBIS = [dt_scr("TBIS%d" % i, [L // 512, 128, L // 128, 512], BF16) for i, L in enumerate(Ls)]
        FH = [dt_scr("FH%d" % i, [L, 1024], BF16) for i, L in enumerate(Ls)]
        GR = [dt_scr("GR%d" % i, [L, 512]) for i, L in enumerate(Ls)]
        GS = [dt_scr("GS%d" % i, [L, 512]) for i, L in enumerate(Ls)]
        PC = dt_scr("PC", [1536, T])
        ZZ = dt_scr("ZZ", [MW, T])
        X1 = dt_scr("X1", [MW, T])
        ZT = dt_scr("ZT", [T, MW], BF16)
        YR = dt_scr("YR", [T, MW], BF16)
        YS = dt_scr("YS", [T, MW], BF16)

    PS = fw.psum

    cst = Phase(fw, "cst")
    ident_f = cst.sb([128, 128], F32, "identf")
    ident_b = cst.sb([128, 128], BF16, "identb")
    jrev_b = cst.sb([128, 128], BF16, "jrevb")
    jrev_f = cst.sb([128, 128], F32, "jrevf")
    mask_f = cst.sb([128, 128], F32, "maskf")
    rst_f = cst.sb([128, 512], F32, "rstf")
    ones_b = cst.sb([128, 128], BF16, "onesb")
    fw.dma('sp', ident_f, c_ident)
    fw.dma('sp', jrev_f, c_jrev)
    fw.dma('sp', mask_f, c_mask)
    fw.dma('sp', rst_f, c_rst)
    cmask_f = cst.sb([128, 512], F32, "cmaskf")
    cmask_b = cst.sb([128, 4, 128], BF16, "cmaskb")
    rmask_f = cst.sb([128, 4], F32, "rmaskf")
    fw.dma('sp', cmask_f, c_cmask)
    fw.dma('sp', rmask_f, c_rmask)
    fw.copy(cmask_b.rearrange("p c j -> p (c j)"), cmask_f)
    rst32_4 = cst.sb([128, 1024], F32, "rst32_4")
    rst128_4 = cst.sb([128, 1024], F32, "rst128_4")
    mask128_f = cst.sb([128, 128], F32, "mask128")
    sel_f = cst.sb([8, 1024], F32, "self")
    sel_b = cst.sb([8, 8, 128], BF16, "selb")
    fw.dma('sp', sel_f, c_sel)
    fw.copy(sel_b.rearrange("k g m -> k (g m)"), sel_f)
    fw.dma('sp', mask128_f, c_mask128)
    for i_ in range(2):
        fw.dma('sp', rst32_4[:, i_ * 512:(i_ + 1) * 512], c_rst)
        fw.dma('sp', rst128_4[:, i_ * 512:(i_ + 1) * 512], c_rst128)
    fw.copy(ident_b, ident_f)
    fw.copy(jrev_b, jrev_f)
    fw.memset(ones_b, 1.0)

    def colload(ph, vec_ap, n, name):
        t = ph.sb([128, n], F32, name)
        fw.dma('sp', t, vec_ap.rearrange("(j p) -> p j", p=128), allow_slow_non_contiguous=True)
        return t

    def rowbc(ph, vec_ap, n, name):
        t = ph.sb([128, n], F32, name)
        fw.dma('sp', t, vec_ap.rearrange("(o n) -> o n", o=1).partition_broadcast(128))
        return t

    def rsqrt_col(out, ss, scale):
        fw.ts(out, ss, scale, EPS, ALU.mult, ALU.add)
        fw.act(out, out, AF.Sqrt)
        fw.op('dve', lambda e: e.reciprocal(out=out, in_=out), r=[out], w=[out])

    def norm_phase(Hsrc, gamma_ap, dst, dst_rev=None, tag="nrm"):
        with Phase(fw, tag) as ph:
            gcol = colload(ph, gamma_ap, 8, "g")
            hts = [ph.sb([128, 4, D], F32, "ht") for _ in range(2)]
            hns = [ph.sb([128, 4, D], BF16, "hn") for _ in range(2)]
            sq = ph.sb([128, D], F32, "sq")
            ss = [ph.sb([128, 4], F32, "ss") for _ in range(2)]
            rs = [ph.sb([128, 4], F32, "rs") for _ in range(2)]
            stg = [ph.sb([128, 8, 512], BF16, "stg") for _ in range(2)]
            stgr = [ph.sb([128, 8, 512], BF16, "stgr") for _ in range(2)]
            it = 0
            for si, L in enumerate(seqs):
                for t0 in range(0, L, 512):
                    g0 = soff[si] + t0
                    b = it % 2
                    ht, hn, ssb, rsb = hts[b], hns[b], ss[b], rs[b]
                    fw.dma('sp', ht, Hsrc[g0:g0 + 512, :].rearrange("(b p) d -> p b d", p=128))
                    for bb in range(4):
                        fw.act(sq, ht[:, bb, :], AF.Square, accum=ssb[:, bb:bb + 1])
                    rsqrt_col(rsb, ssb, 1.0 / D)
                    for bb in range(4):
                        fw.ts(hn[:, bb, :], ht[:, bb, :], rsb[:, bb:bb + 1], None, ALU.mult)
                    for kc in range(8):
                        ps = PS[kc % 4]
                        for bb in range(4):
                            fw.mm(ps[:, bb * 128:(bb + 1) * 128], hn[:, bb, kc * 128:(kc + 1) * 128], ident_b)
                        fw.act(stg[b][:, kc, :], ps, AF.Copy, scale=gcol[:, kc:kc + 1])
                        if dst_rev is not None:
                            ps2 = PS[4 + kc % 4]
                            for bb in range(4):
                                fw.mm(ps2[:, (3 - bb) * 128:(4 - bb) * 128], hn[:, bb, kc * 128:(kc + 1) * 128], jrev_b)
                            fw.ts(stgr[b][:, kc, :], ps2, gcol[:, kc:kc + 1], None, ALU.mult)
                    fw.dma('pool', dst[:, g0:g0 + 512].rearrange("(kc p) t -> p kc t", p=128), stg[b])
                    if dst_rev is not None:
                        r0 = soff[si] + L - 512 - t0
                        fw.dma('pool', dst_rev[:, r0:r0 + 512].rearrange("(kc p) t -> p kc t", p=128), stgr[b])
                    it += 1

    def load_w(ph, w_ap, K, N, name):
        t = ph.sb([128, K // 128, N], BF16, name)
        for kc in range(K // 128):
            fw.dma('pool', t[:, kc, :], w_ap[kc * 128:(kc + 1) * 128, :])
        return t

    def linear_fm(ph, X, K, wsb, N, epi, tag):
        KC = K // 128
        xas = [ph.sb([128, KC, 512], BF16, tag + "xa") for _ in range(2)]
        it = 0
        pi = 0
        for t0 in range(0, T, 512):
            xa = xas[it % 2]
            fw.dma('sp', xa, X[:, t0:t0 + 512].rearrange("(kc p) t -> p kc t", p=128))
            for oc in range(N // 128):
                ps = PS[pi % 8]
                pi += 1
                for kc in range(KC):
                    fw.mm(ps, wsb[:, kc, oc * 128:(oc + 1) * 128], xa[:, kc, :], start=(kc == 0), stop=(kc == KC - 1))
                epi(oc, ps, t0, it)
            it += 1

    def mixer_a(l):
        with Phase(fw, "pa%d" % l) as ph:
            wsb = load_w(ph, W['w_in'][l, :, OFF_A:OFF_A + 1024], D, 1024, "w")
            bcol = colload(ph, W['b_in'][l, OFF_A:OFF_A + 1024], 8, "b")
            stg = [ph.sb([128, 8, 512], F32, "stg") for _ in range(2)]

            def epi(oc, ps, t0, it):
                fw.act(stg[it % 2][:, oc, :], ps, AF.Identity if oc < 4 else AF.Gelu_apprx_tanh,
                       bias=bcol[:, oc:oc + 1])
                if oc == 7:
                    fw.dma('pool', PA[:, t0:t0 + 512].rearrange("(c p) t -> p c t", p=128), stg[it % 2])
            linear_fm(ph, XN, D, wsb, 1024, epi, "la")
        SEG = min(2048, max(seqs))
        with Phase(fw, "ma%d" % l) as ph:
            cw = ph.sb([128, 4, 4], F32, "cw")
            fw.dma('sp', cw, W['conv_a_w'][l].rearrange("j (c p) -> p j c", p=128), allow_slow_non_contiguous=True)
            cb = colload(ph, W['conv_a_b'][l], 4, "cb")
            gb = colload(ph, W['rglru_b'][l].rearrange("a b n -> (a b n)"), 16, "gb")
            lam = colload(ph, W['rglru_lam'][l].rearrange("a n -> (a n)"), 8, "lam")
            c1 = ph.sb([128, 8], F32, "c1")
            c2 = ph.sb([128, 8], F32, "c2")
            fw.act(c1, lam, AF.Exp, scale=-1.0)
            fw.act(c1, c1, AF.Ln, bias=1.0)
            fw.ts(c2, c1, -16.0, None, ALU.mult)
            fw.ts(c1, c1, -8.0, None, ALU.mult)
            gw = ph.sb([128, 16, 128], BF16, "gw")
            fw.dma('pool', gw, W['rglru_w'][l].rearrange("d g h i j -> i (d g h) j"))
            xp = [ph.sb([128, SEG + 3], F32, "xp") for _ in range(2)]
            xc = ph.sb([128, SEG], F32, "xc")
            xb = ph.sb([128, SEG], BF16, "xb")
            gr = ph.sb([128, SEG], F32, "gr")
            gi = ph.sb([128, SEG], F32, "gi")
            aa = ph.sb([128, SEG], F32, "aa")
            uu = ph.sb([128, SEG], F32, "uu")
            hh = [ph.sb([128, SEG], F32, "hh") for _ in range(2)]
            hf = [ph.sb([128, SEG], F32, "hf") for _ in range(2)]
            yy = [ph.sb([128, SEG], F32, "yy") for _ in range(2)]
            ob = [ph.sb([128, SEG], BF16, "ob") for _ in range(2)]
            carry = ph.sb([128, 1], F32, "carry")
            it = 0
            for si, L in enumerate(seqs):
                SG = min(SEG, L)
                for d in range(2):
                    for cc in range(4):
                        nseg = L // SG
                        order = range(nseg) if d == 0 else range(nseg - 1, -1, -1)
                        fw.memset(carry, 0.0)
                        for sg in order:
                            t0 = sg * SG
                            g0 = soff[si] + t0
                            xpt = xp[it % 2]
                            lo = 1 if t0 == 0 else 0
                            hi = 2 if t0 + SG == L else 0
                            if lo or hi:
                                fw.memset(xpt[:, 0:SG + 3], 0.0)
                            fw.dma('sp', xpt[:, lo:SG + 3 - hi],
                                   PA[cc * 128:(cc + 1) * 128, g0 - 1 + lo:g0 + SG + 2 - hi])
                            X = lambda t: t[:, 0:SG]
                            fw.act(X(xc), xpt[:, 1:SG + 1], AF.Identity, bias=cb[:, cc:cc + 1], scale=cw[:, 1, cc:cc + 1])
                            for j in (0, 2, 3):
                                fw.stt(X(xc), xpt[:, j:j + SG], cw[:, j, cc:cc + 1], X(xc), ALU.mult, ALU.add)
                            fw.copy(X(xb), X(xc), 'act')
                            pi = 0
                            for g, dst in ((0, gr), (1, gi)):
                                col = (d * 2 + g) * 4 + cc
                                for sb_ in range(SG // 512):
                                    ps = PS[pi % 8]
                                    pi += 1
                                    fw.mm(ps, gw[:, col, :], xb[:, sb_ * 512:(sb_ + 1) * 512])
                                    fw.act(dst[:, sb_ * 512:(sb_ + 1) * 512], ps, AF.Sigmoid, bias=gb[:, col:col + 1])
                            dc = d * 4 + cc
                            fw.act(X(aa), X(gr), AF.Exp, scale=c1[:, dc:dc + 1])
                            fw.act(X(uu), X(gr), AF.Exp, scale=c2[:, dc:dc + 1])
                            fw.act(X(uu), X(uu), AF.Sqrt, bias=1.0, scale=-1.0)
                            fw.tt(X(gi), X(gi), X(xc), ALU.mult, eng='pool')
                            fw.tt(X(uu), X(uu), X(gi), ALU.mult)
                            h = X(hh[it % 2])
                            if d == 0:
                                fw.scan(h, X(aa), X(uu), carry)
                                fw.copy(carry, h[:, SG - 1:SG])
                                fw.dma('pool', HF[cc * 128:(cc + 1) * 128, g0:g0 + SG], h)
                            else:
                                fw.scan(h[:, ::-1], X(aa)[:, ::-1], X(uu)[:, ::-1], carry)
                                fw.copy(carry, h[:, 0:1])
                                hft, yt, obt = X(hf[it % 2]), X(yy[it % 2]), X(ob[it % 2])
                                fw.dma('sp', hft, HF[cc * 128:(cc + 1) * 128, g0:g0 + SG])
                                fw.dma('sp', yt, PA[512 + cc * 128:512 + (cc + 1) * 128, g0:g0 + SG])
                                fw.tt(h, h, hft, ALU.add, eng='pool')
                                fw.tt(obt, h, yt, ALU.mult)
                                fw.dma('pool', BR[0][cc * 128:(cc + 1) * 128, g0:g0 + SG], obt)
                            it += 1
                    if d == 0:
                        fw.barrier()

    def merge_phase(l, Hin, Hout, bids):
        nbr = len(bids)
        with Phase(fw, "mg%d" % l) as ph:
            wg = load_w(ph, W['w_in'][l, :, OFF_G:OFF_G + 4096], D, 4096, "wg")
            wb = [load_w(ph, W['w_branch'][l, i], MW, D, "wb%d" % i) for i in range(4)]
            bcol = colload(ph, W['b_in'][l, OFF_G:OFF_G + 4096], 32, "b")
            xas = [ph.sb([128, 8, 512], BF16, "xa") for _ in range(2)]
            brs = [[ph.sb([128, 4, 512], BF16, "br") for _ in range(4)] for _ in range(2)]
            gsb = [ph.sb([128, 512], F32, "g") for _ in range(2)]
            acc = ph.sb([128, 512], F32, "acc")
            stg = [ph.sb([128, 8, 512], BF16, "stg") for _ in range(2)]
            it = 0
            pi = 0
            for t0 in range(0, T, 512):
                xa = xas[it % 2]
                fw.dma('sp', xa, XN[:, t0:t0 + 512].rearrange("(kc p) t -> p kc t", p=128))
                for i in bids:
                    fw.dma('sp', brs[it % 2][i], BR[i][:, t0:t0 + 512].rearrange("(kc p) t -> p kc t", p=128))
                for oc in range(8):
                    first = True
                    for ii, i in enumerate(bids):
                        pg, pb = PS[pi % 8], PS[(pi + 1) % 8]
                        pi += 2
                        col = i * 8 + oc
                        for kc in range(8):
                            fw.mm(pg, wg[:, kc, col * 128:(col + 1) * 128], xa[:, kc, :], start=(kc == 0), stop=(kc == 7))
                        for kc in range(4):
                            fw.mm(pb, wb[i][:, kc, oc * 128:(oc + 1) * 128], brs[it % 2][i][:, kc, :],
                                  start=(kc == 0), stop=(kc == 3))
                        g = gsb[ii % 2]
                        fw.act(g, pg, AF.Sigmoid, bias=bcol[:, col:col + 1])
                        last = (ii == nbr - 1)
                        dst = stg[it % 2][:, oc, :] if last and not first else acc
                        if first:
                            fw.tt(stg[it % 2][:, oc, :] if last else acc, g, pb, ALU.mult)
                        else:
                            fw.tt(g, g, pb, ALU.mult)
                            fw.tt(dst, acc, g, ALU.add)
                        first = False
                fw.dma('pool', MG[:, t0:t0 + 512].rearrange("(c p) t -> p c t", p=128), stg[it % 2])
                it += 1
        linear_tm_res(l, MG, D, W['w_out'][l], Hin, Hout, "wo")

    def linear_tm_res(l, X, K, w_ap, Hin, Hout, tag):
        KC = K // 128
        with Phase(fw, "%s%d" % (tag, l)) as ph:
            wsb = load_w(ph, w_ap, K, D, "w")
            xas = [ph.sb([128, KC, 512], BF16, "xa") for _ in range(2)]
            hts = [ph.sb([128, 4, D], F32, "ht") for _ in range(2)]
            it = 0
            pi = 0
            for t0 in range(0, T, 512):
                xa, ht = xas[it % 2], hts[it % 2]
                fw.dma('sp', xa, X[:, t0:t0 + 512].rearrange("(kc p) t -> p kc t", p=128))
                fw.dma('sp', ht, Hin[t0:t0 + 512, :].rearrange("(b p) d -> p b d", p=128))
                for bb in range(4):
                    for cg in range(2):
                        ps = PS[pi % 8]
                        pi += 1
                        for kc in range(KC):
                            fw.mm(ps, xa[:, kc, bb * 128:(bb + 1) * 128], wsb[:, kc, cg * 512:(cg + 1) * 512],
                                  start=(kc == 0), stop=(kc == KC - 1))
                        fw.tt(ht[:, bb, cg * 512:(cg + 1) * 512], ht[:, bb, cg * 512:(cg + 1) * 512], ps, ALU.add)
                fw.dma('pool', Hout[t0:t0 + 512, :].rearrange("(b p) d -> p b d", p=128), ht)
                it += 1

    def final_phase(Hsrc):
        with Phase(fw, "fin") as ph:
            gbc = rowbc(ph, W['final_norm'], D, "g")
            hts = [ph.sb([128, 4, D], F32, "ht") for _ in range(2)]
            sq = ph.sb([128, D], F32, "sq")
            ss = [ph.sb([128, 4], F32, "ss") for _ in range(2)]
            rs = [ph.sb([128, 4], F32, "rs") for _ in range(2)]
            it = 0
            for t0 in range(0, T, 512):
                ht, ssb, rsb = hts[it % 2], ss[it % 2], rs[it % 2]
                fw.dma('sp', ht, Hsrc[t0:t0 + 512, :].rearrange("(b p) d -> p b d", p=128))
                for bb in range(4):
                    fw.act(sq, ht[:, bb, :], AF.Square, accum=ssb[:, bb:bb + 1])
                rsqrt_col(rsb, ssb, 1.0 / D)
                for bb in range(4):
                    fw.stt(ht[:, bb, :], ht[:, bb, :], rsb[:, bb:bb + 1], gbc, ALU.mult, ALU.mult)
                fw.dma('pool', ytok[t0:t0 + 512, :].rearrange("(b p) d -> p b d", p=128), ht)
                it += 1


    lbt = cst.sb([128, 4, 4], F32, "lbt")
    omlt = cst.sb([128, 4, 4], F32, "omlt")
    nomlt = cst.sb([128, 4, 4], F32, "nomlt")
    if 'D' in feat:
        et = cst.sb([128, 4, 4], F32, "lbe")
        st_ = cst.sb([128, 4], F32, "lbs")
        fw.dma('sp', et, W['hgrn_lb_logits'].rearrange("l (c p) -> p l c", p=128), allow_slow_non_contiguous=True)
        fw.act(et, et, AF.Exp)
        fw.rsum(st_, et.rearrange("p l c -> p c l"))
        fw.op('dve', lambda e: e.reciprocal(out=st_, in_=st_), r=[st_], w=[st_])
        fw.memset(lbt, 0.0)
        for li in range(1, 4):
            fw.tt(et[:, li, :], et[:, li, :], st_, ALU.mult)
            fw.tt(lbt[:, li, :], lbt[:, li - 1, :], et[:, li, :], ALU.add)
        fw.ts(omlt, lbt, -1.0, 1.0, ALU.mult, ALU.add)
        fw.ts(nomlt, lbt, 1.0, -1.0, ALU.mult, ALU.add)

    def mixer_gla(l, mx):
        isB = (mx == 'B')
        base = OFF_B if isB else OFF_D
        CH = 128 if isB else 32
        NCH = 512 // CH
        rst4 = rst128_4 if isB else rst32_4
        for dr in range(2):
            X = XN if dr == 0 else XNR
            with Phase(fw, "g%s%d%d" % (mx, l, dr)) as ph:
                if isB:
                    wq = load_w(ph, W['w_in'][l, :, base:base + 512], D, 512, "wq")
                    wk = load_w(ph, W['w_in'][l, :, base + 512:base + 1024], D, 512, "wk")
                    wv = load_w(ph, W['w_in'][l, :, base + 1024:base + 1536], D, 512, "wv")
                    vb_ap = W['b_in'][l, base + 1024:base + 1536]
                    wo_ap, ob_ap = W['w_in'][l, :, base + 1536:base + 2048], W['b_in'][l, base + 1536:base + 2048]
                    g0c = base + 2048 + dr * 8
                    wgs = load_w(ph, W['w_in'][l, :, g0c:g0c + 8], D, 8, "wgs")
                    gb8 = ph.sb([8, 1], F32, "gb8")
                    fw.dma('sp', gb8, W['b_in'][l, g0c:g0c + 8].rearrange("(p o) -> p o", o=1))
                    gsb = ph.sb([8, 512], F32, "gsb")
                    ghi = [ph.sb([8, 512], BF16, "ghi") for _ in range(2)]
                    glo = [ph.sb([8, 512], BF16, "glo") for _ in range(2)]
                    qb = colload(ph, W['b_in'][l, base:base + 512], 4, "qb")
                    kb = colload(ph, W['b_in'][l, base + 512:base + 1024], 4, "kb")
                else:
                    wq = load_w(ph, W['w_in'][l, :, base:base + 512], D, 512, "wq")
                    wk = load_w(ph, W['w_in'][l, :, base + 512 + dr * 512:base + 1024 + dr * 512], D, 512, "wf")
                    wv = load_w(ph, W['w_in'][l, :, base + 1536:base + 2048], D, 512, "wv")
                    vb_ap = W['b_in'][l, base + 1536:base + 2048]
                    wo_ap, ob_ap = W['w_in'][l, :, base + 2048:base + 2560], W['b_in'][l, base + 2048:base + 2560]
                    qb = colload(ph, W['b_in'][l, base:base + 512], 4, "qb")
                    kb = colload(ph, W['b_in'][l, base + 512 + dr * 512:base + 1024 + dr * 512], 4, "fb")
                vbias = rowbc(ph, vb_ap, 512, "vb")
                if dr == 0:
                    wo = load_w(ph, wo_ap, D, 512, "wo")
                    obias = rowbc(ph, ob_ap, 512, "ob")
                    ogs = [ph.sb([128, 4, 512], F32, "ogs") for _ in range(2)]
                xas = [ph.sb([128, 8, 512], BF16, "xa") for _ in range(2)]
                mk2 = lambda dt_, nm: [ph.sb([128, 2, 512], dt_, nm) for _ in range(2)]
                qraws, kks, css, aas, ees, sgs = (mk2(F32, "qraw"), mk2(F32, "kk"), mk2(F32, "cs"), mk2(F32, "aa"),
                                                  mk2(F32, "ee"), mk2(F32, "sg") if not isB else None)
                qts, kts, khb = mk2(BF16, "qt"), mk2(BF16, "kt"), mk2(BF16, "kh")
                khs = [ph.sb([128, 2, 4, 128], BF16, "khs") for _ in range(2)]
                dls = [ph.sb([128, 2, NCH], F32, "dl") for _ in range(2)]
                vsts = [ph.sb([128, 4, 4, 132], BF16, "vst") for _ in range(2)]
                for v_ in vsts:
                    fw.memset(v_, 0.0)
                    fw.memset(v_[:, :, :, 128:129], 1.0)
                F = lambda t: t.rearrange("p h t -> p (h t)")
                CV = lambda t: t.rearrange("p h (c t) -> p (h c) t", t=CH)
                it = 0
                pi = 0
                ih = 0
                for t0 in range(0, T, 512):
                    xa = xas[it % 2]
                    fw.dma('sp', xa, X[:, t0:t0 + 512].rearrange("(kc p) t -> p kc t", p=128))

                    def proj(wt, col):
                        nonlocal pi
                        ps = PS[pi % 6]
                        pi += 1
                        for kc in range(8):
                            fw.mm(ps, wt(kc, col), xa[:, kc, :], start=(kc == 0), stop=(kc == 7))
                        return ps
                    if isB:
                        pgs = PS[pi % 6]
                        pi += 1
                        for kc in range(8):
                            fw.mm(pgs[0:8, :], wgs[:, kc, :], xa[:, kc, :], start=(kc == 0), stop=(kc == 7))
                        fw.act(gsb, pgs[0:8, :], AF.Identity, bias=gb8)
                        gh, gl = ghi[it % 2], glo[it % 2]
                        fw.copy(gh, gsb)
                        fw.tt(gl, gsb, gh, ALU.subtract)
                    for hb in range(2):
                        s_ = ih % 2
                        ih += 1
                        qraw, kk, cs, aa, ee = qraws[s_], kks[s_], css[s_], aas[s_], ees[s_]
                        qt, kt, kh, khst, dl = qts[s_], kts[s_], khb[s_], khs[s_], dls[s_]
                        for hh_ in range(2):
                            h = 2 * hb + hh_
                            pq = proj(lambda kc, h_: wq[:, kc, h_ * 128:(h_ + 1) * 128], h)
                            if isB:
                                fw.act(qraw[:, hh_, :], pq, AF.Identity, bias=qb[:, h:h + 1])
                            else:
                                fw.act(qraw[:, hh_, :], pq, AF.Silu, bias=qb[:, h:h + 1])
                            pk = proj(lambda kc, h_: wk[:, kc, h_ * 128:(h_ + 1) * 128], h)
                            if isB:
                                fw.ts(kk[:, hh_, :], pk, kb[:, h:h + 1], 128.0 ** -0.5, ALU.add, ALU.mult)
                                pig = PS[pi % 6]
                                pi += 1
                                fw.mm(pig, sel_b[:, h, :], gh, start=True, stop=False)
                                fw.mm(pig, sel_b[:, h, :], gl, start=False, stop=True)
                                fw.copy(aa[:, hh_, :], pig, 'act')
                                pfg = PS[pi % 6]
                                pi += 1
                                fw.mm(pfg, sel_b[:, 4 + h, :], gh, start=True, stop=False)
                                fw.mm(pfg, sel_b[:, 4 + h, :], gl, start=False, stop=True)
                                fw.act(ee[:, hh_, :], pfg, AF.Exp, scale=-1.0)
                            else:
                                sg = sgs[s_]
                                fw.act(sg[:, hh_, :], pk, AF.Sigmoid, bias=kb[:, h:h + 1])
                                fw.ts(ee[:, hh_, :], sg[:, hh_, :], omlt[:, l, h:h + 1], lbt[:, l, h:h + 1], ALU.mult, ALU.add)
                                fw.ts(kk[:, hh_, :], sg[:, hh_, :], nomlt[:, l, h:h + 1], omlt[:, l, h:h + 1], ALU.mult, ALU.add,
                                      eng='pool')
                        r4 = rst4[:, 0:1024]
                        if isB:
                            fw.act(F(ee), F(ee), AF.Ln, bias=1.0)
                            fw.scan(F(cs), r4, F(ee), 0.0, ALU.mult, ALU.subtract)
                            fw.tt(F(aa), F(aa), F(cs), ALU.subtract)
                        else:
                            fw.act(F(ee), F(ee), AF.Ln)
                            fw.scan(F(cs), r4, F(ee), 0.0, ALU.mult, ALU.add)
                            fw.ts(F(aa), F(cs), -1.0, None, ALU.mult, eng='pool')
                        fw.act(F(ee), F(cs), AF.Exp)
                        fw.tt(F(qt), F(qraw), F(ee), ALU.mult)
                        fw.act(F(ee), F(aa), AF.Exp)
                        fw.tt(F(kt), F(kk), F(ee), ALU.mult)
                        fw.tt(CV(aa), CV(aa), CV(cs)[:, :, CH - 1:CH].to_broadcast([128, 2 * NCH, CH]), ALU.add)
                        fw.act(F(aa), F(aa), AF.Exp)
                        fw.tt(F(kh), F(kk), F(aa), ALU.mult, eng='pool')
                        fw.act(dl.rearrange("p h c -> p (h c)"), CV(cs)[:, :, CH - 1], AF.Exp)
                        for hh_ in range(2):
                            pt_ = PS[6 + hh_ % 2]
                            for bb in range(4):
                                fw.mm(pt_[:, bb * 128:(bb + 1) * 128], kh[:, hh_, bb * 128:(bb + 1) * 128], ident_b)
                            fw.copy(khst[:, hh_, :, :].rearrange("p b d -> p (b d)"), pt_, 'act')
                        h2 = slice(2 * hb, 2 * hb + 2)
                        fw.dma('pool', QT[dr, h2, :, t0:t0 + 512].rearrange("h p t -> p h t"), qt)
                        fw.dma('pool', KT[dr, h2, :, t0:t0 + 512].rearrange("h p t -> p h t"), kt)
                        for hh_ in range(2):
                            fw.dma('pool', KH[dr, 2 * hb + hh_, t0:t0 + 512, :].rearrange("(b p) d -> p b d", p=128),
                                   khst[:, hh_, :, :])
                        fw.dma('pool', DL[dr, h2, :, t0 // CH:t0 // CH + NCH].rearrange("h p c -> p h c"), dl)
                    vst = vsts[it % 2]
                    for bb in range(4):
                        pv = PS[6 + bb % 2]
                        for kc in range(8):
                            fw.mm(pv, xa[:, kc, bb * 128:(bb + 1) * 128], wv[:, kc, :], start=(kc == 0), stop=(kc == 7))
                        fw.tt(vst[:, bb, :, 0:128], pv.rearrange("p (h c) -> p h c", c=128),
                              vbias.rearrange("p (h c) -> p h c", c=128), ALU.add)
                    fw.dma('pool', VV[dr, t0:t0 + 512, :, :].rearrange("(b p) h c -> p b h c", p=128), vst)
                    if dr == 0:
                        og = ogs[it % 2]
                        for bb in range(4):
                            po = PS[6 + bb % 2]
                            for kc in range(8):
                                fw.mm(po, xa[:, kc, bb * 128:(bb + 1) * 128], wo[:, kc, :], start=(kc == 0), stop=(kc == 7))
                            fw.tt(og[:, bb, :], po, obias, ALU.add)
                            fw.act(og[:, bb, :], og[:, bb, :], AF.Sigmoid)
                        fw.dma('pool', OG[t0:t0 + 512, :].rearrange("(b p) c -> p b c", p=128), og)
                    it += 1
        if int(os.environ.get('GLA_STAGE', '9')) < 2:
            return
        NSUB = 128 // CH
        with Phase(fw, "c%s%d" % (mx, l)) as ph:
            S32 = [[ph.sb([128, 132], F32, "S") for _ in range(4)] for _ in range(2)]
            RB = 2 * NSUB + 2
            Sb = [[[ph.sb([128, 132], BF16, "Sb") for _ in range(RB)] for _ in range(4)] for _ in range(2)]
            qTs = [ph.sb([128, 512], BF16, "q") for _ in range(3)]
            kTs = [ph.sb([128, 512], BF16, "k") for _ in range(3)]
            khs = [ph.sb([128, 4, 128], BF16, "kh") for _ in range(3)]
            vts = [ph.sb([128, 4, 132], BF16, "v") for _ in range(3)]
            dls = [ph.sb([128, 4 * NSUB], F32, "dl") for _ in range(3)]
            atms = [ph.sb([128, 128], BF16, "atm") for _ in range(2)]
            osts = [ph.sb([128, 4, 132], F32, "ost") for _ in range(2)]
            qzs = [ph.sb([128, 4, NSUB, 128], BF16, "qz") for _ in range(2)]
            khzs = [ph.sb([128, 4, NSUB, 128], BF16, "khz") for _ in range(2)]
            mk = mask128_f if isB else mask_f
            it = 0
            ib = 0
            for si, L in enumerate(seqs):
                for dr in range(2):
                    for h in range(4):
                        fw.memset(S32[dr][h], 0.0)
                        fw.memset(Sb[dr][h][0], 0.0)
                cstep = [[0] * 4 for _ in range(2)]
                for t0 in range(0, L, 512):
                    g0 = soff[si] + t0
                    for dr in range(2):
                        for h in range(4):
                            qT, kT, khst, vt, dl, ost = qTs[it % 3], kTs[it % 3], khs[it % 3], vts[it % 3], dls[it % 3], osts[it % 2]
                            qz, khz = qzs[it % 2], khzs[it % 2]
                            it += 1
                            fw.dma('sp', qT, QT[dr, h, :, g0:g0 + 512])
                            fw.dma('sp', kT, KT[dr, h, :, g0:g0 + 512])
                            fw.dma('sp', khst, KH[dr, h, g0:g0 + 512, :].rearrange("(b p) d -> p b d", p=128))
                            fw.dma('sp', vt, VV[dr, g0:g0 + 512, h, :].rearrange("(b p) c -> p b c", p=128))
                            fw.dma('sp', dl, DL[dr, h, :, g0 // CH:g0 // CH + 4 * NSUB])
                            S = S32[dr][h]
                            qv = qT.rearrange("p (b j) -> p b j", j=128)
                            if NSUB > 1:
                                fw.tt(qz if dr == 0 else qz[:, :, :, ::-1],
                                      qv.unsqueeze(2).to_broadcast([128, 4, NSUB, 128]),
                                      cmask_b.unsqueeze(1).to_broadcast([128, 4, NSUB, 128]), ALU.mult)
                                fw.tt(khz, khst.unsqueeze(2).to_broadcast([128, 4, NSUB, 128]),
                                      rmask_f.unsqueeze(1).unsqueeze(3).to_broadcast([128, 4, NSUB, 128]), ALU.mult,
                                      eng='pool')
                            elif dr == 1:
                                fw.copy(qz[:, :, 0, ::-1], qv, 'pool')
                            units = []

                            def stage1(bb):
                                nonlocal ib
                                pat = PS[ib % 2]
                                po = PS[2 + ib % 2]
                                pd = (PS[4 + 2 * (ib % 2)], PS[5 + 2 * (ib % 2)])
                                atm = atms[ib % 2]
                                ib += 1
                                sl = slice(bb * 128, (bb + 1) * 128)
                                fw.mm(pat[:, 0:128], kT[:, sl], qT[:, sl])
                                dsl = []
                                for c in range(NSUB):
                                    dd = pd[c // 2][:, (c % 2) * 256:(c % 2) * 256 + 132]
                                    fw.mm(dd, khz[:, bb, c, :] if NSUB > 1 else khst[:, bb, :], vt[:, bb, :])
                                    dsl.append(dd)
                                fw.tt(atm if dr == 0 else atm[:, ::-1], pat[:, 0:128], mk, ALU.mult)
                                sbs = []
                                for c in range(NSUB):
                                    cidx = bb * NSUB + c
                                    cs_ = cstep[dr][h]
                                    cstep[dr][h] += 1
                                    sbs.append(Sb[dr][h][cs_ % RB])
                                    fw.stt(S, S, dl[:, cidx:cidx + 1], dsl[c], ALU.mult, ALU.add)
                                    fw.copy(Sb[dr][h][(cs_ + 1) % RB], S, 'act')
                                return (bb, po, atm, sbs, sl)

                            def stage2(u):
                                bb, po, atm, sbs, sl = u
                                fw.mm(po[:, 0:132], atm, vt[:, bb, :], start=True, stop=False)
                                for c in range(NSUB):
                                    if NSUB > 1 or dr == 1:
                                        lq = qz[:, bb, c, :]
                                    else:
                                        lq = qT[:, sl]
                                    fw.mm(po[:, 0:132], lq, sbs[c], start=False, stop=(c == NSUB - 1))
                                fw.copy(ost[:, bb if dr == 0 else 3 - bb, :], po[:, 0:132], 'act')
                            prev = stage1(0)
                            for bb in range(4):
                                nxt_ = stage1(bb + 1) if bb < 3 else None
                                stage2(prev)
                                prev = nxt_
                            go = g0 if dr == 0 else soff[si] + L - 512 - t0
                            fw.dma('pool', OO[dr, go:go + 512, h, :].rearrange("(b p) c -> p b c", p=128), ost)
        if int(os.environ.get('GLA_STAGE', '9')) < 3:
            return
        with Phase(fw, "m%s%d" % (mx, l)) as ph:
            gain = rowbc(ph, W['mlstm_norm' if isB else 'hgrn_norm'][l], 512, "gn")
            ofs = [ph.sb([128, 4, 4, 132], F32, "of") for _ in range(2)]
            obv = [ph.sb([128, 4, 4, 132], F32, "ob") for _ in range(2)]
            ogt = [ph.sb([128, 4, 512], F32, "og") for _ in range(2)]
            obr = ph.sb([128, 4, 132], F32, "obr")
            hs = ph.sb([128, 4, 128], F32, "hs")
            hb = ph.sb([128, 4, 128], F32, "hb")
            sq = ph.sb([128, 4, 128], F32, "sq")
            rf = ph.sb([128, 4, 1], F32, "rf")
            rb = ph.sb([128, 4, 1], F32, "rb")
            ss = ph.sb([128, 4], F32, "ss")
            rs = ph.sb([128, 4], F32, "rs")
            hsb = ph.sb([128, 512], BF16, "hsb")
            stg = [ph.sb([128, 4, 512], BF16, "stg") for _ in range(2)]
            it = 0
            for si, L in enumerate(seqs):
                for t0 in range(0, L, 512):
                    g0 = soff[si] + t0
                    of, ob_, og, st = ofs[it % 2], obv[it % 2], ogt[it % 2], stg[it % 2]
                    fw.dma('sp', of, OO[0, g0:g0 + 512, :, :].rearrange("(b p) h c -> p b h c", p=128))
                    fw.dma('sp', ob_, OO[1, g0:g0 + 512, :, :].rearrange("(b p) h c -> p b h c", p=128))
                    fw.dma('sp', og, OG[g0:g0 + 512, :].rearrange("(b p) c -> p b c", p=128))
                    for bb in range(4):
                        obr = ob_[:, bb, :, :]
                        if isB:
                            fw.act(rf, of[:, bb, :, 128:129], AF.Abs)
                            fw.ts(rf, rf, 1.0, None, ALU.max)
                            fw.op('dve', lambda e: e.reciprocal(out=rf, in_=rf), r=[rf], w=[rf])
                            fw.act(rb, obr[:, :, 128:129], AF.Abs)
                            fw.ts(rb, rb, 1.0, None, ALU.max)
                            fw.op('dve', lambda e: e.reciprocal(out=rb, in_=rb), r=[rb], w=[rb])
                            fw.tt(hs, of[:, bb, :, 0:128], rf.to_broadcast([128, 4, 128]), ALU.mult)
                            fw.tt(hb, obr[:, :, 0:128], rb.to_broadcast([128, 4, 128]), ALU.mult)
                            fw.tt(hs, hs, hb, ALU.add)
                        else:
                            fw.tt(hs, of[:, bb, :, 0:128], obr[:, :, 0:128], ALU.add)
                        fw.tt(sq, hs, hs, ALU.mult)
                        fw.rsum(ss, sq)
                        rsqrt_col(rs, ss, 1.0 / 128)
                        fw.tt(hs, hs, rs.unsqueeze(2).to_broadcast([128, 4, 128]), ALU.mult)
                        fw.tt(hs, hs, gain.rearrange("p (h c) -> p h c", c=128), ALU.mult)
                        fw.tt(hsb.rearrange("p (h c) -> p h c", c=128), hs, og[:, bb, :].rearrange("p (h c) -> p h c", c=128), ALU.mult)
                        pt_ = PS[4 + bb % 2]
                        for h in range(4):
                            fw.mm(pt_[:, h * 128:(h + 1) * 128], hsb[:, h * 128:(h + 1) * 128], ident_b)
                        fw.copy(st[:, :, bb * 128:(bb + 1) * 128], pt_.rearrange("p (h t) -> p h t", t=128), 'act')
                    fw.dma('pool', BR[1 if isB else 3][:, g0:g0 + 512].rearrange("(h p) t -> p h t", p=128), st)
                    it += 1


    def gen_tables(li, L):
        NCn = L // 128
        M1 = (4 * L) // 64
        with Phase(fw, "tg%d" % li) as ph:
            fx = ph.sb([128, 2, NCn], F32, "fx")
            ix = ph.sb([128, 2, NCn], F32, "ix")
            fw.dma('sp', fx, c_fx[li].rearrange("a p n -> p a n"))
            fw.dma('sp', ix, c_ix[li].rearrange("a p n -> p a n"))
            r0 = ph.sb([128, 128], F32, "r0")
            t0r = ph.sb([128, 512], F32, "t0r")
            fw.dma('sp', r0, c_r0)
            fw.dma('sp', t0r, c_t0)
            yrow = ph.sb([128, 512], F32, "yrow")
            A = ph.sb([128, 2048], F32, "A")
            Ai = ph.sb([128, 2048], I32, "Ai")
            Bq = ph.sb([128, 2048], F32, "Bq")
            mi = ph.sb([128, 2048], I32, "mi")
            mc = ph.sb([128, 2048], I32, "mc")
            ms = ph.sb([128, 2048], I32, "ms")
            snb = [ph.sb([128, 2048], BF16, "sn") for _ in range(2)]
            csb = [ph.sb([128, 2048], BF16, "cs") for _ in range(2)]
            cnt = [0]

            def gen(dC, dS, xhi, xlo, yr, NA, NJ):
                n = NA * NJ
                v = lambda t: t[:, 0:n].rearrange("p (a j) -> p a j", j=NJ)
                xb = lambda x: x.unsqueeze(2).to_broadcast([128, NA, NJ])
                yb = yr.unsqueeze(1).to_broadcast([128, NA, NJ])
                sn, cs_ = snb[cnt[0] % 2], csb[cnt[0] % 2]
                cnt[0] += 1
                fw.tt(v(Ai), yb, xb(xhi), ALU.mult)
                fw.tss(Ai[:, 0:n], Ai[:, 0:n], M1 - 1, ALU.bitwise_and)
                fw.tt(v(Bq), yb, xb(xlo), ALU.mult, eng='pool')
                fw.stt(mi[:, 0:n], Ai[:, 0:n], 64.0, Bq[:, 0:n], ALU.mult, ALU.add)
                fw.tss(mc[:, 0:n], mi[:, 0:n], L, ALU.add, eng='pool')
                fw.tss(mc[:, 0:n], mc[:, 0:n], 4 * L - 1, ALU.bitwise_and)
                fw.tss(ms[:, 0:n], mi[:, 0:n], 4 * L - 1, ALU.bitwise_and)
                fw.act(sn[:, 0:n], ms[:, 0:n], AF.Sin, bias=PI_S, scale=-math.pi / (2 * L))
                fw.act(cs_[:, 0:n], mc[:, 0:n], AF.Sin, bias=PI_S, scale=-math.pi / (2 * L))
                fw.dma('pool', dS, v(sn))
                fw.dma('pool', dC, v(cs_))

            NA = min(16, NCn)
            for kc in range(NCn):
                fw.ts(yrow[:, 0:128], r0, float(256 * kc), None, ALU.add)
                for sub in range(NCn // NA):
                    a = slice(sub * NA, (sub + 1) * NA)
                    gen(TBFC[li][kc, :, a, :], TBFS[li][kc, :, a, :], fx[:, 0, a], fx[:, 1, a], yrow[:, 0:128], NA, 128)
            NA = min(4, NCn)
            for tt in range(L // 512):
                fw.ts(yrow, t0r, float(512 * tt), None, ALU.add)
                for sub in range(NCn // NA):
                    a = slice(sub * NA, (sub + 1) * NA)
                    gen(TBIC[li][tt, :, a, :], TBIS[li][tt, :, a, :], ix[:, 0, a], ix[:, 1, a], yrow, NA, 512)

    def col64(ph, vec_ap, name):
        t = ph.sb([64, 1], F32, name)
        fw.dma('sp', t, vec_ap.rearrange("(j p) -> p j", p=64), allow_slow_non_contiguous=True)
        return t

    def hilo(ph, src, shape, name):
        hi = ph.sb(shape, BF16, name + "h")
        lo = ph.sb(shape, BF16, name + "l")
        fw.copy(hi, src)
        fw.tt(lo, src, hi, ALU.subtract)
        return hi, lo

    def hyena_filter(l, li, L):
        with Phase(fw, "hf%d_%d" % (l, li)) as ph:
            w1 = ph.sb([33, 64], F32, "w1")
            w2 = ph.sb([64, 64], F32, "w2")
            fw.dma('sp', w1, W['hyena_w1'][l])
            fw.dma('sp', w2, W['hyena_w2'][l])
            w1h, w1l = hilo(ph, w1, [33, 64], "w1")
            w2h, w2l = hilo(ph, w2, [64, 64], "w2")
            w3b = ph.sb([64, 1024], BF16, "w3")
            fw.dma('pool', w3b, W['hyena_w3'][l])
            b1 = col64(ph, W['hyena_b1'][l], "b1")
            b2 = col64(ph, W['hyena_b2'][l], "b2")
            f1 = col64(ph, W['hyena_freq1'][l], "f1")
            f2 = col64(ph, W['hyena_freq2'][l], "f2")
            fw.ts(f1, f1, 1.0 / (2 * math.pi), None, ALU.mult)
            fw.ts(f2, f2, 1.0 / (2 * math.pi), None, ALU.mult)
            ratebc = rowbc(ph, c_rate, 512, "rate")
            ntc = colload(ph, c_nt[li], L // 128, "nt")
            ft = ph.sb([33, 512], F32, "ft")
            fth = ph.sb([33, 512], BF16, "fth")
            ftl = ph.sb([33, 512], BF16, "ftl")
            q = ph.sb([64, 512], F32, "q")
            qi = ph.sb([64, 512], I32, "qi")
            hd = ph.sb([64, 512], F32, "hd")
            hdh = ph.sb([64, 512], BF16, "hdh")
            hdl = ph.sb([64, 512], BF16, "hdl")
            dec = ph.sb([128, 512], F32, "dec")
            h0 = ph.sb([128, 512], F32, "h0")
            h1 = ph.sb([128, 512], F32, "h1")
            stg = [ph.sb([128, 4, 1024], BF16, "stg") for _ in range(2)]

            def sin_layer(ps, bcol, fcol, dst):
                fw.ts(q, ps, bcol, fcol, ALU.add, ALU.mult)
                fw.copy(qi, q)
                fw.tt(q, q, qi, ALU.subtract)
                fw.act(dst, q, AF.Sin, scale=6.283185)

            it = 0
            for t0 in range(0, L, 512):
                fw.dma('sp', ft, c_feat[li][:, t0:t0 + 512])
                fw.copy(fth, ft)
                fw.tt(ftl, ft, fth, ALU.subtract)
                ps = PS[0][0:64, :]
                fw.mm(ps, w1h, fth, start=True, stop=False)
                fw.mm(ps, w1h, ftl, start=False, stop=False)
                fw.mm(ps, w1l, fth, start=False, stop=True)
                sin_layer(ps, b1, f1, hd)
                fw.copy(hdh, hd)
                fw.tt(hdl, hd, hdh, ALU.subtract)
                ps2 = PS[1][0:64, :]
                fw.mm(ps2, w2h, hdh, start=True, stop=False)
                fw.mm(ps2, w2h, hdl, start=False, stop=False)
                fw.mm(ps2, w2l, hdh, start=False, stop=True)
                sin_layer(ps2, b2, f2, hd)
                fw.copy(hdh, hd)
                st = stg[it % 2]
                for bb in range(4):
                    blk = t0 // 128 + bb
                    p0, p1 = PS[2 + 2 * (bb % 2)], PS[3 + 2 * (bb % 2)]
                    fw.mm(p0, hdh[:, bb * 128:(bb + 1) * 128], w3b[:, 0:512])
                    fw.mm(p1, hdh[:, bb * 128:(bb + 1) * 128], w3b[:, 512:1024])
                    fw.act(dec, ratebc, AF.Exp, scale=ntc[:, blk:blk + 1])
                    fw.tt(h0, p0, dec, ALU.mult)
                    fw.tt(h1, p1, dec, ALU.mult)
                    fw.tt(st[:, bb, 0:512], h0, h1, ALU.add)
                    fw.tt(st[:, bb, 512:1024], h0, h1, ALU.subtract)
                fw.dma('pool', FH[li][t0:t0 + 512, :].rearrange("(b p) c -> p b c", p=128), st)
                it += 1

    def mixer_c(l):
        for li, L in enumerate(Ls):
            hyena_filter(l, li, L)
        with Phase(fw, "pc%d" % l) as ph:
            wsb = load_w(ph, W['w_in'][l, :, OFF_C:OFF_C + 1536], D, 1536, "w")
            bcol = colload(ph, W['b_in'][l, OFF_C:OFF_C + 1536], 12, "b")
            stg = [ph.sb([128, 12, 512], F32, "stg") for _ in range(2)]

            def epi(oc, ps, t0, it):
                fw.act(stg[it % 2][:, oc, :], ps, AF.Identity, bias=bcol[:, oc:oc + 1])
                if oc == 11:
                    fw.dma('pool', PC[:, t0:t0 + 512].rearrange("(c p) t -> p c t", p=128), stg[it % 2])
            linear_fm(ph, XN, D, wsb, 1536, epi, "lc")
        with Phase(fw, "hz%d" % l) as ph:
            cw = ph.sb([128, 3, 12], F32, "cw")
            fw.dma('sp', cw, W['hyena_conv_w'][l].rearrange("j (c p) -> p j c", p=128), allow_slow_non_contiguous=True)
            cb = colload(ph, W['hyena_conv_b'][l], 12, "cb")
            xps = [ph.sb([128, 514], F32, "xp") for _ in range(3)]
            cvs = [[ph.sb([128, 512], F32, "cv") for _ in range(3)] for _ in range(2)]
            zts = [ph.sb([128, 512], F32, "z") for _ in range(2)]
            zb = ph.sb([128, 512], BF16, "zb")
            ztt = [ph.sb([128, 4, 128], BF16, "ztt") for _ in range(2)]
            it = 0
            for si, L in enumerate(seqs):
                for t0 in range(0, L, 512):
                    g0 = soff[si] + t0
                    lo = 1 if t0 == 0 else 0
                    hi = 1 if t0 + 512 == L else 0
                    for cc in range(4):
                        cv, zt = cvs[it % 2], zts[it % 2]
                        for qi_ in range(3):
                            xp = xps[qi_]
                            col = qi_ * 4 + cc
                            if lo or hi:
                                fw.memset(xp, 0.0)
                            fw.dma('sp', xp[:, lo:514 - hi], PC[col * 128:(col + 1) * 128, g0 - 1 + lo:g0 + 513 - hi])
                            fw.act(cv[qi_], xp[:, 1:513], AF.Identity, bias=cb[:, col:col + 1], scale=cw[:, 1, col:col + 1])
                            fw.stt(cv[qi_], xp[:, 0:512], cw[:, 0, col:col + 1], cv[qi_], ALU.mult, ALU.add)
                            fw.stt(cv[qi_], xp[:, 2:514], cw[:, 2, col:col + 1], cv[qi_], ALU.mult, ALU.add)
                        fw.tt(zt, cv[2], cv[0], ALU.mult)
                        fw.copy(zb, zt, 'act')
                        fw.dma('pool', ZZ[cc * 128:(cc + 1) * 128, g0:g0 + 512], zt)
                        fw.dma('pool', X1[cc * 128:(cc + 1) * 128, g0:g0 + 512], cv[1])
                        pt_ = PS[it % 2]
                        for bb in range(4):
                            fw.mm(pt_[:, bb * 128:(bb + 1) * 128], zb[:, bb * 128:(bb + 1) * 128], ident_b)
                        zt2 = ztt[it % 2]
                        fw.copy(zt2.rearrange("p b c -> p (b c)"), pt_, 'act')
                        fw.dma('pool', ZT[g0:g0 + 512, cc * 128:(cc + 1) * 128].rearrange("(b p) c -> p b c", p=128), zt2)
                        it += 1
        for li, L in enumerate(Ls):
            NCn = L // 128
            sis = [i for i, Lx in enumerate(seqs) if Lx == L]
            with Phase(fw, "hg%d_%d" % (l, li)) as ph:
                mov = ph.sb([128, NCn, 512], BF16, "mov")
                tbs = [ph.sb([128, NCn, 128], BF16, "tb") for _ in range(2)]
                gst = [ph.sb([128, 512], F32, "g") for _ in range(2)]
                it = 0
                for (tb_d, c0, dst) in ((TBFC[li], 0, GR[li]), (TBFS[li], 512, GS[li])):
                    fw.dma('sp', mov, FH[li][:, c0:c0 + 512].rearrange("(n p) c -> p n c", p=128))
                    for kc in range(NCn):
                        tb = tbs[it % 2]
                        fw.dma('sp', tb, tb_d[kc])
                        ps = PS[it % 4]
                        for n_ in range(NCn):
                            fw.mm(ps, tb[:, n_, :], mov[:, n_, :], start=(n_ == 0), stop=(n_ == NCn - 1))
                        fw.act(gst[it % 2], ps, AF.Copy, scale=1.0 / L)
                        fw.dma('pool', dst[kc * 128:(kc + 1) * 128, :], gst[it % 2])
                        it += 1
            with Phase(fw, "hx%d_%d" % (l, li)) as ph:
                zr = [ph.sb([128, NCn, 512], BF16, "zr") for _ in sis]
                tbc = [ph.sb([128, NCn, 128], BF16, "tbc") for _ in range(2)]
                tbsn = [ph.sb([128, NCn, 128], BF16, "tbs") for _ in range(2)]
                grt = [ph.sb([128, 512], F32, "gr") for _ in range(2)]
                gs_t = [ph.sb([128, 512], F32, "gs") for _ in range(2)]
                t1 = ph.sb([128, 512], F32, "t1")
                t2 = ph.sb([128, 512], F32, "t2")
                yrt = [ph.sb([128, 512], BF16, "yr") for _ in range(2)]
                yst = [ph.sb([128, 512], BF16, "ys") for _ in range(2)]
                for j, si in enumerate(sis):
                    fw.dma('sp', zr[j], ZT[soff[si]:soff[si] + L, :].rearrange("(n p) c -> p n c", p=128))
                it = 0
                for kc in range(NCn):
                    tc_, ts_ = tbc[kc % 2], tbsn[kc % 2]
                    fw.dma('sp', tc_, TBFC[li][kc])
                    fw.dma('sp', ts_, TBFS[li][kc])
                    gr, gs = grt[kc % 2], gs_t[kc % 2]
                    fw.dma('sp', gr, GR[li][kc * 128:(kc + 1) * 128, :])
                    fw.dma('sp', gs, GS[li][kc * 128:(kc + 1) * 128, :])
                    for j, si in enumerate(sis):
                        pr, ps_ = PS[(2 * it) % 8], PS[(2 * it + 1) % 8]
                        for n_ in range(NCn):
                            fw.mm(pr, tc_[:, n_, :], zr[j][:, n_, :], start=(n_ == 0), stop=(n_ == NCn - 1))
                        for n_ in range(NCn):
                            fw.mm(ps_, ts_[:, n_, :], zr[j][:, n_, :], start=(n_ == 0), stop=(n_ == NCn - 1))
                        yr, ys = yrt[it % 2], yst[it % 2]
                        fw.tt(t1, pr, gr, ALU.mult)
                        fw.tt(t2, ps_, gs, ALU.mult)
                        fw.tt(yr, t1, t2, ALU.subtract)
                        fw.tt(t1, pr, gs, ALU.mult)
                        fw.tt(t2, ps_, gr, ALU.mult)
                        fw.tt(ys, t1, t2, ALU.add)
                        r0_ = soff[si] + kc * 128
                        fw.dma('pool', YR[r0_:r0_ + 128, :], yr)
                        fw.dma('pool', YS[r0_:r0_ + 128, :], ys)
                        it += 1
            with Phase(fw, "hi%d_%d" % (l, li)) as ph:
                skipc = colload(ph, W['hyena_skip'][l], 4, "skip")
                yrr = ph.sb([128, NCn, 512], BF16, "yrr")
                ysr = ph.sb([128, NCn, 512], BF16, "ysr")
                KS = min(8, NCn)
                tcs = [ph.sb([128, KS, 512], BF16, "tc") for _ in range(2)]
                tss_ = [ph.sb([128, KS, 512], BF16, "ts") for _ in range(2)]
                zt = [ph.sb([128, 512], F32, "z") for _ in range(2)]
                x1t = [ph.sb([128, 512], F32, "x1") for _ in range(2)]
                ob = [ph.sb([128, 512], BF16, "ob") for _ in range(2)]
                it = 0
                ie = 0
                for si in sis:
                    fw.dma('sp', yrr, YR[soff[si]:soff[si] + L, :].rearrange("(n p) c -> p n c", p=128))
                    fw.dma('sp', ysr, YS[soff[si]:soff[si] + L, :].rearrange("(n p) c -> p n c", p=128))
                    for tt in range(L // 512):
                        g0 = soff[si] + tt * 512
                        pacc = [PS[4 * (tt % 2) + cc] for cc in range(4)]
                        for sub in range(NCn // KS):
                            tc_, ts_ = tcs[it % 2], tss_[it % 2]
                            it += 1
                            a = slice(sub * KS, (sub + 1) * KS)
                            fw.dma('sp', tc_, TBIC[li][tt, :, a, :])
                            fw.dma('sp', ts_, TBIS[li][tt, :, a, :])
                            for k_ in range(KS):
                                kc = sub * KS + k_
                                for cc in range(4):
                                    fw.mm(pacc[cc], yrr[:, kc, cc * 128:(cc + 1) * 128], tc_[:, k_, :],
                                          start=(kc == 0), stop=False)
                                    fw.mm(pacc[cc], ysr[:, kc, cc * 128:(cc + 1) * 128], ts_[:, k_, :],
                                          start=False, stop=(kc == NCn - 1))
                        for cc in range(4):
                            z_, x_, o_ = zt[ie % 2], x1t[ie % 2], ob[ie % 2]
                            ie += 1
                            fw.dma('sp', z_, ZZ[cc * 128:(cc + 1) * 128, g0:g0 + 512])
                            fw.dma('sp', x_, X1[cc * 128:(cc + 1) * 128, g0:g0 + 512])
                            fw.stt(z_, z_, skipc[:, cc:cc + 1], pacc[cc], ALU.mult, ALU.add)
                            fw.tt(o_, z_, x_, ALU.mult)
                            fw.dma('pool', BR[2][cc * 128:(cc + 1) * 128, g0:g0 + 512], o_)

    def ffn_phase(l, Hin, Hout):
        with Phase(fw, "ff%d" % l) as ph:
            wu = load_w(ph, W['w_up'][l], D, 2 * DFF, "wu")
            wd = load_w(ph, W['w_down'][l], DFF, D, "wd")
            cw = ph.sb([128, 3, 32], F32, "cw")
            fw.dma('sp', cw, W['ffn_conv_w'][l].rearrange("j (c p) -> p j c", p=128), allow_slow_non_contiguous=True)
            cb = colload(ph, W['ffn_conv_b'][l], 32, "cb")
            xas = [ph.sb([128, 8, 512], BF16, "xa") for _ in range(2)]
            hts = [ph.sb([128, 4, D], F32, "ht") for _ in range(2)]
            ca = ph.sb([128, 510], F32, "ca")
            cbt = ph.sb([128, 510], F32, "cbt")
            act_t = [ph.sb([128, 16, 512], BF16, "act") for _ in range(2)]
            it = 0
            pi = 0
            for si, L in enumerate(seqs):
                for t0 in range(0, L, 510):
                    nout = min(510, L - t0)
                    g0 = soff[si] + t0
                    xa, ht, at = xas[it % 2], hts[it % 2], act_t[it % 2]
                    lo = 1 if t0 == 0 else 0
                    ncol = min(512, L - (t0 - 1)) - lo
                    if lo or (lo + ncol) < 512:
                        fw.memset(xa, 0.0)
                    fw.dma('sp', xa[:, :, lo:lo + ncol],
                           XN[:, g0 - 1 + lo:g0 - 1 + lo + ncol].rearrange("(kc p) t -> p kc t", p=128))
                    nb = (nout + 127) // 128
                    for bb in range(nb):
                        r = min(128, nout - bb * 128)
                        fw.dma('sp', ht[0:r, bb, :], Hin[g0 + bb * 128:g0 + bb * 128 + r, :])
                    for oc in range(16):
                        for half, dst in ((0, ca), (1, cbt)):
                            col = oc + 16 * half
                            ps = PS[pi % 4]
                            pi += 1
                            for kc in range(8):
                                fw.mm(ps, wu[:, kc, col * 128:(col + 1) * 128], xa[:, kc, :], start=(kc == 0), stop=(kc == 7))
                            fw.act(dst, ps[:, 1:511], AF.Identity, bias=cb[:, col:col + 1], scale=cw[:, 1, col:col + 1])
                            fw.stt(dst, ps[:, 0:510], cw[:, 0, col:col + 1], dst, ALU.mult, ALU.add)
                            fw.stt(dst, ps[:, 2:512], cw[:, 2, col:col + 1], dst, ALU.mult, ALU.add)
                        fw.act(ca, ca, AF.Gelu_apprx_tanh)
                        fw.tt(at[:, oc, 0:510], ca, cbt, ALU.mult)
                    for bb in range(nb):
                        r = min(128, nout - bb * 128)
                        for cg in range(2):
                            ps = PS[4 + pi % 4]
                            pi += 1
                            for kc in range(16):
                                fw.mm(ps[0:r, :], at[:, kc, bb * 128:bb * 128 + r], wd[:, kc, cg * 512:(cg + 1) * 512],
                                      start=(kc == 0), stop=(kc == 15))
                            fw.tt(ht[0:r, bb, cg * 512:(cg + 1) * 512], ht[0:r, bb, cg * 512:(cg + 1) * 512], ps[0:r, :], ALU.add)
                        fw.dma('pool', Hout[g0 + bb * 128:g0 + bb * 128 + r, :], ht[0:r, bb, :])
                    it += 1

    def ple_phase(l, Hin, Hout):
        with Phase(fw, "pl%d" % l) as ph:
            wg = load_w(ph, W['w_ple_gate'][l], D, D, "wg")
            wp = load_w(ph, W['w_ple'][l], DPLE, D, "wp")
            xas = [ph.sb([128, 8, 512], BF16, "xa") for _ in range(2)]
            hts = [ph.sb([128, 4, D], F32, "ht") for _ in range(2)]
            pts = [ph.sb([128, 4, DPLE], F32, "pt") for _ in range(2)]
            pb = ph.sb([128, 4, DPLE], BF16, "pb")
            pT = ph.sb([128, 2, 512], BF16, "pT")
            gs = [ph.sb([128, 512], F32, "gs") for _ in range(2)]
            it = 0
            pi = 0
            for t0 in range(0, T, 512):
                xa, ht, pt = xas[it % 2], hts[it % 2], pts[it % 2]
                fw.dma('sp', xa, XN[:, t0:t0 + 512].rearrange("(kc p) t -> p kc t", p=128))
                fw.dma('sp', ht, Hin[t0:t0 + 512, :].rearrange("(b p) d -> p b d", p=128))
                fw.dma('sp', pt, ptok[l, t0:t0 + 512, :].rearrange("(b p) d -> p b d", p=128))
                fw.copy(pb, pt, 'act')
                for kc2 in range(2):
                    ps = PS[pi % 8]
                    pi += 1
                    for bb in range(4):
                        fw.mm(ps[:, bb * 128:(bb + 1) * 128], pb[:, bb, kc2 * 128:(kc2 + 1) * 128], ident_b)
                    fw.copy(pT[:, kc2, :], ps)
                for bb in range(4):
                    for cg in range(2):
                        pg, pp = PS[pi % 8], PS[(pi + 1) % 8]
                        pi += 2
                        for kc in range(8):
                            fw.mm(pg, xa[:, kc, bb * 128:(bb + 1) * 128], wg[:, kc, cg * 512:(cg + 1) * 512],
                                  start=(kc == 0), stop=(kc == 7))
                        for kc in range(2):
                            fw.mm(pp, pT[:, kc, bb * 128:(bb + 1) * 128], wp[:, kc, cg * 512:(cg + 1) * 512],
                                  start=(kc == 0), stop=(kc == 1))
                        g = gs[pi % 2]
                        fw.act(g, pg, AF.Sigmoid)
                        fw.tt(g, g, pp, ALU.mult)
                        fw.tt(ht[:, bb, cg * 512:(cg + 1) * 512], ht[:, bb, cg * 512:(cg + 1) * 512], g, ALU.add)
                fw.dma('pool', Hout[t0:t0 + 512, :].rearrange("(b p) d -> p b d", p=128), ht)
                it += 1

    if 'C' in feat:
        for li, L in enumerate(Ls):
            gen_tables(li, L)
    Hb = [dt_scr("H2", [T, D]), dt_scr("H3", [T, D])]
    cur = xtok
    for l in range(depth):
        hi = 0
        def nxt():
            return [H[0], H[1], Hb[0], Hb[1]][[i for i in range(4) if [H[0], H[1], Hb[0], Hb[1]][i] is not cur][0]]
        need_rev = ('B' in feat) or ('D' in feat)
        norm_phase(cur, W['norm_mix'][l], XN, XNR if need_rev else None, "nm%d" % l)
        bids = []
        if 'A' in feat:
            mixer_a(l)
            bids.append(0)
        if 'B' in feat:
            mixer_gla(l, 'B')
            bids.append(1)
        if 'C' in feat:
            mixer_c(l)
            bids.append(2)
        if 'D' in feat:
            mixer_gla(l, 'D')
            bids.append(3)
        h1 = nxt()
        merge_phase(l, cur, h1, bids)
        cur = h1
        if 'ffn' in feat:
            norm_phase(cur, W['norm_ffn'][l], XN, None, "nf%d" % l)
            h2 = nxt()
            ffn_phase(l, cur, h2)
            cur = h2
        if 'ple' in feat:
            norm_phase(cur, W['norm_ple'][l], XN, None, "np%d" % l)
            h3 = nxt()
            ple_phase(l, cur, h3)
            cur = h3
    final_phase(cur)
    fw.barrier()
    cst.es.close()
    fw.es.close()
    return nc, fw


def make_consts(seqs):
    Ls = sorted(set(seqs))
    out = _make_consts0()
    j = np.arange(128, dtype=np.float32)
    out['c_r0'] = np.tile((2 * j + 1)[None, :], (128, 1)).astype(np.float32)
    out['c_t0'] = np.tile(np.arange(512, dtype=np.float32)[None, :], (128, 1))
    out['c_rate'] = np.abs(np.linspace(math.log(1e-2) / 0.3, math.log(1e-2) / 1.5, 512, dtype=np.float32)).astype(np.float32)
    for i, L in enumerate(Ls):
        t = np.linspace(0.0, 1.0, L, dtype=np.float32)[:, None]
        omega = (np.float32(2.0 * math.pi / L) * np.arange(L, dtype=np.float32))[:, None]
        bands = np.linspace(1e-4, 15, 16, dtype=np.float32)[None, :]
        feat = np.concatenate([t, np.cos(omega * bands), -np.sin(omega * bands)], axis=-1).astype(np.float32)
        out['c_feat%d' % i] = np.ascontiguousarray(feat.T)
        out['c_nt%d' % i] = np.ascontiguousarray(-t[:, 0])
        p = np.arange(128)[:, None]
        n = np.arange(L // 128)[None, :] * 128 + p
        out['c_fx%d' % i] = np.stack([n // 64, n % 64]).astype(np.float32)
        x = 2 * n + 1
        out['c_ix%d' % i] = np.stack([x // 64, x % 64]).astype(np.float32)
    return out


def _make_consts0():
    ident = np.eye(128, dtype=np.float32)
    jrev = np.ascontiguousarray(ident[::-1])
    s = np.arange(128)
    mask = ((s[:, None] <= s[None, :]) & ((s[:, None] // 32) == (s[None, :] // 32))).astype(np.float32)
    rst = np.ones((128, 512), np.float32)
    rst[:, ::32] = 0.0
    cm = np.zeros((128, 4, 128), np.float32)
    rm = np.zeros((128, 4), np.float32)
    for c in range(4):
        cm[:, c, 32 * c:32 * c + 32] = 1.0
        rm[32 * c:32 * c + 32, c] = 1.0
    rst128 = np.ones((128, 512), np.float32)
    rst128[:, ::128] = 0.0
    mask128 = (s[:, None] <= s[None, :]).astype(np.float32)
    return dict(c_ident=ident, c_jrev=jrev, c_mask=mask, c_rst=rst, c_cmask=cm.reshape(128, 512), c_rmask=rm,
                c_rst128=rst128, c_mask128=mask128,
                c_sel=np.repeat(np.eye(8, dtype=np.float32), 128, axis=1))


WNAMES = ['norm_mix', 'w_in', 'b_in', 'conv_a_w', 'conv_a_b', 'rglru_w', 'rglru_b', 'rglru_lam', 'mlstm_norm',
          'hyena_conv_w', 'hyena_conv_b', 'hyena_w1', 'hyena_b1', 'hyena_freq1', 'hyena_w2', 'hyena_b2',
          'hyena_freq2', 'hyena_w3', 'hyena_skip', 'hgrn_lb_logits', 'hgrn_norm', 'w_branch', 'w_out', 'norm_ffn',
          'w_up', 'ffn_conv_w', 'ffn_conv_b', 'w_down', 'norm_ple', 'w_ple_gate', 'w_ple', 'final_norm']

_CACHE = {}


def run(inputs, seqs_per_core, depth, groups, dbg=(), feat=('A', 'B', 'C', 'D', 'ffn', 'ple')):
    key = (tuple(seqs_per_core), depth, tuple(dbg), tuple(feat))
    if key not in _CACHE:
        _CACHE[key] = build(list(seqs_per_core), depth, dbg, feat)
    nc, fw = _CACHE[key]
    consts = make_consts(seqs_per_core)
    in_maps = []
    for g in groups:
        xs = np.concatenate([np.asarray(inputs['x_' + n][b], np.float32) for (n, b) in g], axis=0)
        ps = np.concatenate([np.asarray(inputs['p_' + n][:depth, b], np.float32) for (n, b) in g], axis=1)
        m = {"xtok": np.ascontiguousarray(xs), "ptok": np.ascontiguousarray(ps)}
        for k in WNAMES:
            a = np.asarray(inputs[k], np.float32)
            m[k] = a if k == 'final_norm' or k == 'hgrn_lb_logits' else a[:depth]
        m.update(consts)
        in_maps.append(m)
    if os.environ.get('K_TRACE'):
        res = run_bass_kernel_spmd(nc, in_maps, core_ids=list(range(len(groups))), trace=True)
        print("EXEC_NS", res.exec_time_ns)
    else:
        res = run_bass_kernel_spmd(nc, in_maps, core_ids=list(range(len(groups))))
    return res.results


def kernel(**inputs):
    groups = [[('prompt', 2 * c), ('prompt', 2 * c + 1), ('sample', c // 4)] for c in range(8)]
    res = run(inputs, (4096, 4096, 8192), 4, groups)
    yp = np.zeros((16, 4096, D), np.float32)
    ys = np.zeros((2, 8192, D), np.float32)
    for c in range(8):
        y = res[c]["ytok"]
        yp[2 * c] = y[0:4096]
        yp[2 * c + 1] = y[4096:8192]
        if c % 4 == 0:
            ys[c // 4] = y[8192:16384]
    return (yp, ys)
```

```python
import math
import os
from contextlib import ExitStack
import numpy as np
import ml_dtypes
import concourse.bass as bass
import concourse.mybir as mybir
from concourse.bass_utils import run_bass_kernel_spmd

F32, BF16 = mybir.dt.float32, mybir.dt.bfloat16
I32 = mybir.dt.int32
PI_S = 3.1415925
AF = mybir.ActivationFunctionType
ALU = mybir.AluOpType
AX = mybir.AxisListType

D = 1024
MW = 512
DPLE = 256
DFF = 2048
OFF_A, OFF_B, OFF_C, OFF_D = 0, 1024, 1024 + 2064, 1024 + 2064 + 1536
OFF_G = OFF_D + 2560
DIN = OFF_G + 4096
EPS = 1e-6
NS = 16


class FW:
    def __init__(self, nc):
        self.nc = nc
        self.es = ExitStack()
        self.eng = {'pe': nc.tensor, 'act': nc.scalar, 'dve': nc.vector, 'pool': nc.gpsimd, 'sp': nc.sync}
        self.sem = {e: self.es.enter_context(nc.semaphore("pg_" + e)) for e in ('pe', 'act', 'dve', 'pool')}
        self.cnt = {e: 0 for e in self.sem}
        self.dsem = {q: [self.es.enter_context(nc.semaphore("dq_%s_%d" % (q, i))) for i in range(NS)]
                     for q in ('sp', 'pool', 'act')}
        self.dcnt = {q: 0 for q in self.dsem}
        self.known = {e: {} for e in self.eng}
        self.semobj = {}
        for e, s in self.sem.items():
            self.semobj[('c', e)] = s
        for q, l in self.dsem.items():
            for i, s in enumerate(l):
                self.semobj[('d', q, i)] = s
        self.lastw = {}
        self.readers = {}
        self.dram = set()
        self.psum = [self.es.enter_context(nc.psum_tensor("psb%d" % i, [128, 512], F32))[:] for i in range(8)]
        self.ninst = 0

    def _key(self, ap):
        return ap if isinstance(ap, str) else ap.name

    def _wait(self, e, events):
        kn = self.known[e]
        for (sk, v) in events:
            if sk == ('c', e) and e == 'pe':
                continue
            if kn.get(sk, 0) >= v:
                continue
            self.eng[e].wait_ge(self.semobj[sk], v)
            kn[sk] = v

    def _deps(self, r, w):
        ev = []
        for a in r:
            k = self._key(a)
            if k in self.dram:
                continue
            if k in self.lastw:
                ev.append(self.lastw[k])
        for a in w:
            k = self._key(a)
            if k in self.dram:
                continue
            if k in self.lastw:
                ev.append(self.lastw[k])
            ev.extend(self.readers.get(k, ()))
        return ev

    def _record(self, ev, r, w):
        for a in r:
            k = self._key(a)
            if k in self.dram:
                continue
            self.readers.setdefault(k, []).append(ev)
        for a in w:
            k = self._key(a)
            if k in self.dram:
                continue
            self.lastw[k] = ev
            self.readers[k] = []

    def op(self, e, fn, r=(), w=()):
        self._wait(e, self._deps(r, w))
        ins = fn(self.eng[e])
        self.cnt[e] += 1
        ins.then_inc(self.sem[e], 1)
        self._record((('c', e), self.cnt[e]), r, w)
        self.ninst += 1

    def dma(self, q, out, in_, **kw):
        j = self.dcnt[q]
        self.dcnt[q] += 1
        si, rnd = j % NS, j // NS
        ev = list(self._deps([in_], [out]))
        if rnd > 0:
            ev.append((('d', q, si), 16 * rnd))
        self._wait(q, ev)
        self.eng[q].dma_start(out=out, in_=in_, **kw).then_inc(self.dsem[q][si], 16)
        self._record((('d', q, si), 16 * (rnd + 1)), [in_], [out])
        self.ninst += 1

    def barrier(self):
        evs = [(('c', e), self.cnt[e]) for e in self.cnt if self.cnt[e] > 0]
        for q in self.dsem:
            n = self.dcnt[q]
            for si in range(NS):
                cntq = (n - si + NS - 1) // NS if n > si else 0
                if cntq > 0:
                    evs.append((('d', q, si), 16 * cntq))
        for e in self.eng:
            self._wait(e, evs)
        self.lastw = {}
        self.readers = {}

    def mm(self, out, lhsT, rhs, start=True, stop=True):
        self.op('pe', lambda e: e.matmul(out, lhsT, rhs, start=start, stop=stop), r=[lhsT, rhs], w=[out])

    def act(self, out, in_, func, bias=None, scale=None, accum=None, extra_r=()):
        kw = {}
        r = [in_] + list(extra_r)
        if bias is not None:
            kw['bias'] = bias
            if not isinstance(bias, (int, float)):
                r.append(bias)
        if scale is not None:
            kw['scale'] = scale
            if not isinstance(scale, (int, float)):
                r.append(scale)
        w = [out]
        if accum is not None:
            kw['accum_out'] = accum
            w.append(accum)
        self.op('act', lambda e: e.activation(out=out, in_=in_, func=func, **kw), r=r, w=w)

    def tt(self, out, a, b, op, eng='dve'):
        self.op(eng, lambda e: e.tensor_tensor(out=out, in0=a, in1=b, op=op), r=[a, b], w=[out])

    def ts(self, out, a, s1, s2=None, op0=ALU.mult, op1=None, eng='dve'):
        r = [a] + [s for s in (s1, s2) if s is not None and not isinstance(s, (int, float))]
        if op1 is None:
            self.op(eng, lambda e: e.tensor_scalar(out=out, in0=a, scalar1=s1, scalar2=None, op0=op0), r=r, w=[out])
        else:
            self.op(eng, lambda e: e.tensor_scalar(out=out, in0=a, scalar1=s1, scalar2=s2, op0=op0, op1=op1),
                    r=r, w=[out])

    def stt(self, out, a, s, b, op0, op1, eng='dve'):
        r = [a, b] + ([] if isinstance(s, (int, float)) else [s])
        self.op(eng, lambda e: e.scalar_tensor_tensor(out=out, in0=a, scalar=s, in1=b, op0=op0, op1=op1),
                r=r, w=[out])

    def copy(self, out, in_, eng='dve'):
        if eng == 'act':
            self.op('act', lambda e: e.copy(out=out, in_=in_), r=[in_], w=[out])
        else:
            self.op(eng, lambda e: e.tensor_copy(out=out, in_=in_), r=[in_], w=[out])

    def memset(self, ap, v, eng='dve'):
        self.op(eng, lambda e: e.memset(ap, v), r=[], w=[ap])

    def scan(self, out, d0, d1, init, op0=ALU.mult, op1=ALU.add):
        r = [d0, d1] + ([] if isinstance(init, (int, float)) else [init])
        self.op('dve', lambda e: e.tensor_tensor_scan(out=out, data0=d0, data1=d1, initial=init, op0=op0, op1=op1),
                r=r, w=[out])

    def tss(self, out, in_, scalar, op, eng='dve'):
        self.op(eng, lambda e: e.tensor_single_scalar(out=out, in_=in_, scalar=scalar, op=op), r=[in_], w=[out])

    def rsum(self, out, in_):
        self.op('dve', lambda e: e.reduce_sum(out=out, in_=in_, axis=AX.X), r=[in_], w=[out])


PHASE_LOG = []


class Phase:
    def __init__(self, fw, tag):
        self.fw, self.tag, self.es, self.n = fw, tag, ExitStack(), 0

    def __enter__(self):
        return self

    def sb(self, shape, dt=F32, name=None):
        self.n += 1
        nm = "%s_%s%d" % (self.tag, name or "t", self.n)
        return self.es.enter_context(self.fw.nc.sbuf_tensor(nm, list(shape), dt))[:]

    def __exit__(self, *a):
        self.fw.barrier()
        self.es.close()
        PHASE_LOG.append((self.tag, self.fw.cnt['pe'], self.fw.cnt['act'], self.fw.cnt['dve']))
        return False


def build(seqs, depth, dbg=(), feat=('A', 'B', 'C', 'D', 'ffn', 'ple')):
    T = sum(seqs)
    soff = [sum(seqs[:i]) for i in range(len(seqs))]
    assert all(L % 512 == 0 for L in seqs)
    Ls = sorted(set(seqs))
    nc = bass.Bass("TRN2", target_bir_lowering=False)
    fw = FW(nc)
    outs = {}

    def dt_in(name, shape, dt=F32):
        a = nc.dram_tensor(name, list(shape), dt, kind="ExternalInput").ap()
        fw.dram.add(a.name)
        return a

    def dt_scr(name, shape, dt=F32):
        kind = "ExternalOutput" if name in dbg else "Internal"
        a = nc.dram_tensor(name, list(shape), dt, kind=kind).ap()
        fw.dram.add(a.name)
        return a

    xtok = dt_in("xtok", [T, D])
    ptok = dt_in("ptok", [depth, T, DPLE])
    W = {}
    wshapes = dict(
        norm_mix=[depth, D], w_in=[depth, D, DIN], b_in=[depth, DIN], conv_a_w=[depth, 4, MW], conv_a_b=[depth, MW],
        rglru_w=[depth, 2, 2, 4, 128, 128], rglru_b=[depth, 2, 2, MW], rglru_lam=[depth, 2, MW],
        mlstm_norm=[depth, MW], hyena_conv_w=[depth, 3, 1536], hyena_conv_b=[depth, 1536],
        hyena_w1=[depth, 33, 64], hyena_b1=[depth, 64], hyena_freq1=[depth, 64], hyena_w2=[depth, 64, 64],
        hyena_b2=[depth, 64], hyena_freq2=[depth, 64], hyena_w3=[depth, 64, 1024], hyena_skip=[depth, MW],
        hgrn_lb_logits=[4, MW], hgrn_norm=[depth, MW], w_branch=[depth, 4, MW, D], w_out=[depth, D, D],
        norm_ffn=[depth, D], w_up=[depth, D, 2 * DFF], ffn_conv_w=[depth, 3, 2 * DFF], ffn_conv_b=[depth, 2 * DFF],
        w_down=[depth, DFF, D], norm_ple=[depth, D], w_ple_gate=[depth, D, D], w_ple=[depth, DPLE, D],
        final_norm=[D])
    for k, s in wshapes.items():
        W[k] = dt_in(k, s)
    c_ident = dt_in("c_ident", [128, 128])
    c_jrev = dt_in("c_jrev", [128, 128])
    c_mask = dt_in("c_mask", [128, 128])
    c_rst = dt_in("c_rst", [128, 512])
    c_cmask = dt_in("c_cmask", [128, 512])
    c_rmask = dt_in("c_rmask", [128, 4])
    c_rst128 = dt_in("c_rst128", [128, 512])
    c_sel = dt_in("c_sel", [8, 1024])
    c_mask128 = dt_in("c_mask128", [128, 128])
    c_r0 = dt_in("c_r0", [128, 128])
    c_t0 = dt_in("c_t0", [128, 512])
    c_rate = dt_in("c_rate", [512])
    c_feat = [dt_in("c_feat%d" % i, [33, L]) for i, L in enumerate(Ls)]
    c_nt = [dt_in("c_nt%d" % i, [L]) for i, L in enumerate(Ls)]
    c_fx = [dt_in("c_fx%d" % i, [2, 128, L // 128]) for i, L in enumerate(Ls)]
    c_ix = [dt_in("c_ix%d" % i, [2, 128, L // 128]) for i, L in enumerate(Ls)]
    ytok = nc.dram_tensor("ytok", [T, D], F32, kind="ExternalOutput").ap()
    fw.dram.add(ytok.name)

    H = [dt_scr("H0", [T, D]), dt_scr("H1", [T, D])]
    XN = dt_scr("XN", [D, T], BF16)
    XNR = dt_scr("XNR", [D, T], BF16)
    PA = dt_scr("PA", [1024, T])
    HF = dt_scr("HF", [MW, T])
    BR = [dt_scr("BR%d" % i, [MW, T], BF16) for i in range(4)]
    MG = dt_scr("MG", [D, T], BF16)
    QT = dt_scr("QT", [2, 4, 128, T], BF16)
    KT = dt_scr("KT", [2, 4, 128, T], BF16)
    KH = dt_scr("KH", [2, 4, T, 128], BF16)
    VV = dt_scr("VV", [2, T, 4, 132], BF16)
    DL = dt_scr("DL", [2, 4, 128, T // 32])
    OO = dt_scr("OO", [2, T, 4, 132])
    OG = dt_scr("OG", [T, MW])
    if 'C' in feat:
        TBFC = [dt_scr("TBFC%d" % i, [L // 128, 128, L // 128, 128], BF16) for i, L in enumerate(Ls)]
        TBFS = [dt_scr("TBFS%d" % i, [L // 128, 128, L // 128, 128], BF16) for i, L in enumerate(Ls)]
        TBIC = [dt_scr("TBIC%d" % i, [L // 512, 128, L // 128, 512], BF16) for i, L in enumerate(Ls)]
        TBIS = [dt_scr("TBIS%d" % i, [L // 512, 128, L // 128, 512], BF16) for i, L in enumerate(Ls)]
        FH = [dt_scr("FH%d" % i, [L, 1024], BF16) for i, L in enumerate(Ls)]
        GR = [dt_scr("GR%d" % i, [L, 512]) for i, L in enumerate(Ls)]
        GS = [dt_scr("GS%d" % i, [L, 512]) for i, L in enumerate(Ls)]
        PC = dt_scr("PC", [1536, T])
        ZZ = dt_scr("ZZ", [MW, T])
        X1 = dt_scr("X1", [MW, T])
        ZT = dt_scr("ZT", [T, MW], BF16)
        YR = dt_scr("YR", [T, MW], BF16)
        YS = dt_scr("YS", [T, MW], BF16)

    PS = fw.psum

    cst = Phase(fw, "cst")
    ident_f = cst.sb([128, 128], F32, "identf")
    ident_b = cst.sb([128, 128], BF16, "identb")
    jrev_b = cst.sb([128, 128], BF16, "jrevb")
    jrev_f = cst.sb([128, 128], F32, "jrevf")
    mask_f = cst.sb([128, 128], F32, "maskf")
    rst_f = cst.sb([128, 512], F32, "rstf")
    ones_b = cst.sb([128, 128], BF16, "onesb")
    fw.dma('sp', ident_f, c_ident)
    fw.dma('sp', jrev_f, c_jrev)
    fw.dma('sp', mask_f, c_mask)
    fw.dma('sp', rst_f, c_rst)
    cmask_f = cst.sb([128, 512], F32, "cmaskf")
    cmask_b = cst.sb([128, 4, 128], BF16, "cmaskb")
    rmask_f = cst.sb([128, 4], F32, "rmaskf")
    fw.dma('sp', cmask_f, c_cmask)
    fw.dma('sp', rmask_f, c_rmask)
    fw.copy(cmask_b.rearrange("p c j -> p (c j)"), cmask_f)
    rst32_4 = cst.sb([128, 1024], F32, "rst32_4")
    rst128_4 = cst.sb([128, 1024], F32, "rst128_4")
    mask128_f = cst.sb([128, 128], F32, "mask128")
    sel_f = cst.sb([8, 1024], F32, "self")
    sel_b = cst.sb([8, 8, 128], BF16, "selb")
    fw.dma('sp', sel_f, c_sel)
    fw.copy(sel_b.rearrange("k g m -> k (g m)"), sel_f)
    fw.dma('sp', mask128_f, c_mask128)
    for i_ in range(2):
        fw.dma('sp', rst32_4[:, i_ * 512:(i_ + 1) * 512], c_rst)
        fw.dma('sp', rst128_4[:, i_ * 512:(i_ + 1) * 512], c_rst128)
    fw.copy(ident_b, ident_f)
    fw.copy(jrev_b, jrev_f)
    fw.memset(ones_b, 1.0)

    def colload(ph, vec_ap, n, name):
        t = ph.sb([128, n], F32, name)
        fw.dma('sp', t, vec_ap.rearrange("(j p) -> p j", p=128), allow_slow_non_contiguous=True)
        return t

    def rowbc(ph, vec_ap, n, name):
        t = ph.sb([128, n], F32, name)
        fw.dma('sp', t, vec_ap.rearrange("(o n) -> o n", o=1).partition_broadcast(128))
        return t

    def rsqrt_col(out, ss, scale):
        fw.ts(out, ss, scale, EPS, ALU.mult, ALU.add)
        fw.act(out, out, AF.Sqrt)
        fw.op('dve', lambda e: e.reciprocal(out=out, in_=out), r=[out], w=[out])

    def norm_phase(Hsrc, gamma_ap, dst, dst_rev=None, tag="nrm"):
        with Phase(fw, tag) as ph:
            gcol = colload(ph, gamma_ap, 8, "g")
            hts = [ph.sb([128, 4, D], F32, "ht") for _ in range(2)]
            hns = [ph.sb([128, 4, D], BF16, "hn") for _ in range(2)]
            sq = ph.sb([128, D], F32, "sq")
            ss = [ph.sb([128, 4], F32, "ss") for _ in range(2)]
            rs = [ph.sb([128, 4], F32, "rs") for _ in range(2)]
            stg = [ph.sb([128, 8, 512], BF16, "stg") for _ in range(2)]
            stgr = [ph.sb([128, 8, 512], BF16, "stgr") for _ in range(2)]
            it = 0
            for si, L in enumerate(seqs):
                for t0 in range(0, L, 512):
                    g0 = soff[si] + t0
                    b = it % 2
                    ht, hn, ssb, rsb = hts[b], hns[b], ss[b], rs[b]
                    fw.dma('sp', ht, Hsrc[g0:g0 + 512, :].rearrange("(b p) d -> p b d", p=128))
                    for bb in range(4):
                        fw.act(sq, ht[:, bb, :], AF.Square, accum=ssb[:, bb:bb + 1])
                    rsqrt_col(rsb, ssb, 1.0 / D)
                    for bb in range(4):
                        fw.ts(hn[:, bb, :], ht[:, bb, :], rsb[:, bb:bb + 1], None, ALU.mult)
                    for kc in range(8):
                        ps = PS[kc % 4]
                        for bb in range(4):
                            fw.mm(ps[:, bb * 128:(bb + 1) * 128], hn[:, bb, kc * 128:(kc + 1) * 128], ident_b)
                        fw.act(stg[b][:, kc, :], ps, AF.Copy, scale=gcol[:, kc:kc + 1])
                        if dst_rev is not None:
                            ps2 = PS[4 + kc % 4]
                            for bb in range(4):
                                fw.mm(ps2[:, (3 - bb) * 128:(4 - bb) * 128], hn[:, bb, kc * 128:(kc + 1) * 128], jrev_b)
                            fw.ts(stgr[b][:, kc, :], ps2, gcol[:, kc:kc + 1], None, ALU.mult)
                    fw.dma('pool', dst[:, g0:g0 + 512].rearrange("(kc p) t -> p kc t", p=128), stg[b])
                    if dst_rev is not None:
                        r0 = soff[si] + L - 512 - t0
                        fw.dma('pool', dst_rev[:, r0:r0 + 512].rearrange("(kc p) t -> p kc t", p=128), stgr[b])
                    it += 1

    def load_w(ph, w_ap, K, N, name):
        t = ph.sb([128, K // 128, N], BF16, name)
        for kc in range(K // 128):
            fw.dma('pool', t[:, kc, :], w_ap[kc * 128:(kc + 1) * 128, :])
        return t

    def linear_fm(ph, X, K, wsb, N, epi, tag):
        KC = K // 128
        xas = [ph.sb([128, KC, 512], BF16, tag + "xa") for _ in range(2)]
        it = 0
        pi = 0
        for t0 in range(0, T, 512):
            xa = xas[it % 2]
            fw.dma('sp', xa, X[:, t0:t0 + 512].rearrange("(kc p) t -> p kc t", p=128))
            for oc in range(N // 128):
                ps = PS[pi % 8]
                pi += 1
                for kc in range(KC):
                    fw.mm(ps, wsb[:, kc, oc * 128:(oc + 1) * 128], xa[:, kc, :], start=(kc == 0), stop=(kc == KC - 1))
                epi(oc, ps, t0, it)
            it += 1

    def mixer_a(l):
        with Phase(fw, "pa%d" % l) as ph:
            wsb = load_w(ph, W['w_in'][l, :, OFF_A:OFF_A + 1024], D, 1024, "w")
            bcol = colload(ph, W['b_in'][l, OFF_A:OFF_A + 1024], 8, "b")
            stg = [ph.sb([128, 8, 512], F32, "stg") for _ in range(2)]

            def epi(oc, ps, t0, it):
                fw.act(stg[it % 2][:, oc, :], ps, AF.Identity if oc < 4 else AF.Gelu_apprx_tanh,
                       bias=bcol[:, oc:oc + 1])
                if oc == 7:
                    fw.dma('pool', PA[:, t0:t0 + 512].rearrange("(c p) t -> p c t", p=128), stg[it % 2])
            linear_fm(ph, XN, D, wsb, 1024, epi, "la")
        SEG = min(2048, max(seqs))
        with Phase(fw, "ma%d" % l) as ph:
            cw = ph.sb([128, 4, 4], F32, "cw")
            fw.dma('sp', cw, W['conv_a_w'][l].rearrange("j (c p) -> p j c", p=128), allow_slow_non_contiguous=True)
            cb = colload(ph, W['conv_a_b'][l], 4, "cb")
            gb = colload(ph, W['rglru_b'][l].rearrange("a b n -> (a b n)"), 16, "gb")
            lam = colload(ph, W['rglru_lam'][l].rearrange("a n -> (a n)"), 8, "lam")
            c1 = ph.sb([128, 8], F32, "c1")
            c2 = ph.sb([128, 8], F32, "c2")
            fw.act(c1, lam, AF.Exp, scale=-1.0)
            fw.act(c1, c1, AF.Ln, bias=1.0)
            fw.ts(c2, c1, -16.0, None, ALU.mult)
            fw.ts(c1, c1, -8.0, None, ALU.mult)
            gw = ph.sb([128, 16, 128], BF16, "gw")
            fw.dma('pool', gw, W['rglru_w'][l].rearrange("d g h i j -> i (d g h) j"))
            xp = [ph.sb([128, SEG + 3], F32, "xp") for _ in range(2)]
            xc = ph.sb([128, SEG], F32, "xc")
            xb = ph.sb([128, SEG], BF16, "xb")
            gr = ph.sb([128, SEG], F32, "gr")
            gi = ph.sb([128, SEG], F32, "gi")
            aa = ph.sb([128, SEG], F32, "aa")
            uu = ph.sb([128, SEG], F32, "uu")
            hh = [ph.sb([128, SEG], F32, "hh") for _ in range(2)]
            hf = [ph.sb([128, SEG], F32, "hf") for _ in range(2)]
            yy = [ph.sb([128, SEG], F32, "yy") for _ in range(2)]
            ob = [ph.sb([128, SEG], BF16, "ob") for _ in range(2)]
            carry = ph.sb([128, 1], F32, "carry")
            it = 0
            for si, L in enumerate(seqs):
                SG = min(SEG, L)
                for d in range(2):
                    for cc in range(4):
                        nseg = L // SG
                        order = range(nseg) if d == 0 else range(nseg - 1, -1, -1)
                        fw.memset(carry, 0.0)
                        for sg in order:
                            t0 = sg * SG
                            g0 = soff[si] + t0
                            xpt = xp[it % 2]
                            lo = 1 if t0 == 0 else 0
                            hi = 2 if t0 + SG == L else 0
                            if lo or hi:
                                fw.memset(xpt[:, 0:SG + 3], 0.0)
                            fw.dma('sp', xpt[:, lo:SG + 3 - hi],
                                   PA[cc * 128:(cc + 1) * 128, g0 - 1 + lo:g0 + SG + 2 - hi])
                            X = lambda t: t[:, 0:SG]
                            fw.act(X(xc), xpt[:, 1:SG + 1], AF.Identity, bias=cb[:, cc:cc + 1], scale=cw[:, 1, cc:cc + 1])
                            for j in (0, 2, 3):
                                fw.stt(X(xc), xpt[:, j:j + SG], cw[:, j, cc:cc + 1], X(xc), ALU.mult, ALU.add)
                            fw.copy(X(xb), X(xc), 'act')
                            pi = 0
                            for g, dst in ((0, gr), (1, gi)):
                                col = (d * 2 + g) * 4 + cc
                                for sb_ in range(SG // 512):
                                    ps = PS[pi % 8]
                                    pi += 1
                                    fw.mm(ps, gw[:, col, :], xb[:, sb_ * 512:(sb_ + 1) * 512])
                                    fw.act(dst[:, sb_ * 512:(sb_ + 1) * 512], ps, AF.Sigmoid, bias=gb[:, col:col + 1])
                            dc = d * 4 + cc
                            fw.act(X(aa), X(gr), AF.Exp, scale=c1[:, dc:dc + 1])
                            fw.act(X(uu), X(gr), AF.Exp, scale=c2[:, dc:dc + 1])
                            fw.act(X(uu), X(uu), AF.Sqrt, bias=1.0, scale=-1.0)
                            fw.tt(X(gi), X(gi), X(xc), ALU.mult, eng='pool')
                            fw.tt(X(uu), X(uu), X(gi), ALU.mult)
                            h = X(hh[it % 2])
                            if d == 0:
                                fw.scan(h, X(aa), X(uu), carry)
                                fw.copy(carry, h[:, SG - 1:SG])
                                fw.dma('pool', HF[cc * 128:(cc + 1) * 128, g0:g0 + SG], h)
                            else:
                                fw.scan(h[:, ::-1], X(aa)[:, ::-1], X(uu)[:, ::-1], carry)
                                fw.copy(carry, h[:, 0:1])
                                hft, yt, obt = X(hf[it % 2]), X(yy[it % 2]), X(ob[it % 2])
                                fw.dma('sp', hft, HF[cc * 128:(cc + 1) * 128, g0:g0 + SG])
                                fw.dma('sp', yt, PA[512 + cc * 128:512 + (cc + 1) * 128, g0:g0 + SG])
                                fw.tt(h, h, hft, ALU.add, eng='pool')
                                fw.tt(obt, h, yt, ALU.mult)
                                fw.dma('pool', BR[0][cc * 128:(cc + 1) * 128, g0:g0 + SG], obt)
                            it += 1
                    if d == 0:
                        fw.barrier()

    def merge_phase(l, Hin, Hout, bids):
        nbr = len(bids)
        with Phase(fw, "mg%d" % l) as ph:
            wg = load_w(ph, W['w_in'][l, :, OFF_G:OFF_G + 4096], D, 4096, "wg")
            wb = [load_w(ph, W['w_branch'][l, i], MW, D, "wb%d" % i) for i in range(4)]
            bcol = colload(ph, W['b_in'][l, OFF_G:OFF_G + 4096], 32, "b")
            xas = [ph.sb([128, 8, 512], BF16, "xa") for _ in range(2)]
            brs = [[ph.sb([128, 4, 512], BF16, "br") for _ in range(4)] for _ in range(2)]
            gsb = [ph.sb([128, 512], F32, "g") for _ in range(2)]
            acc = ph.sb([128, 512], F32, "acc")
            stg = [ph.sb([128, 8, 512], BF16, "stg") for _ in range(2)]
            it = 0
            pi = 0
            for t0 in range(0, T, 512):
                xa = xas[it % 2]
                fw.dma('sp', xa, XN[:, t0:t0 + 512].rearrange("(kc p) t -> p kc t", p=128))
                for i in bids:
                    fw.dma('sp', brs[it % 2][i], BR[i][:, t0:t0 + 512].rearrange("(kc p) t -> p kc t", p=128))
                for oc in range(8):
                    first = True
                    for ii, i in enumerate(bids):
                        pg, pb = PS[pi % 8], PS[(pi + 1) % 8]
                        pi += 2
                        col = i * 8 + oc
                        for kc in range(8):
                            fw.mm(pg, wg[:, kc, col * 128:(col + 1) * 128], xa[:, kc, :], start=(kc == 0), stop=(kc == 7))
                        for kc in range(4):
                            fw.mm(pb, wb[i][:, kc, oc * 128:(oc + 1) * 128], brs[it % 2][i][:, kc, :],
                                  start=(kc == 0), stop=(kc == 3))
                        g = gsb[ii % 2]
                        fw.act(g, pg, AF.Sigmoid, bias=bcol[:, col:col + 1])
                        last = (ii == nbr - 1)
                        dst = stg[it % 2][:, oc, :] if last and not first else acc
                        if first:
                            fw.tt(stg[it % 2][:, oc, :] if last else acc, g, pb, ALU.mult)
                        else:
                            fw.tt(g, g, pb, ALU.mult)
                            fw.tt(dst, acc, g, ALU.add)
                        first = False
                fw.dma('pool', MG[:, t0:t0 + 512].rearrange("(c p) t -> p c t", p=128), stg[it % 2])
                it += 1
        linear_tm_res(l, MG, D, W['w_out'][l], Hin, Hout, "wo")

    def linear_tm_res(l, X, K, w_ap, Hin, Hout, tag):
        KC = K // 128
        with Phase(fw, "%s%d" % (tag, l)) as ph:
            wsb = load_w(ph, w_ap, K, D, "w")
            xas = [ph.sb([128, KC, 512], BF16, "xa") for _ in range(2)]
            hts = [ph.sb([128, 4, D], F32, "ht") for _ in range(2)]
            it = 0
            pi = 0
            for t0 in range(0, T, 512):
                xa, ht = xas[it % 2], hts[it % 2]
                fw.dma('sp', xa, X[:, t0:t0 + 512].rearrange("(kc p) t -> p kc t", p=128))
                fw.dma('sp', ht, Hin[t0:t0 + 512, :].rearrange("(b p) d -> p b d", p=128))
                for bb in range(4):
                    for cg in range(2):
                        ps = PS[pi % 8]
                        pi += 1
                        for kc in range(KC):
                            fw.mm(ps, xa[:, kc, bb * 128:(bb + 1) * 128], wsb[:, kc, cg * 512:(cg + 1) * 512],
                                  start=(kc == 0), stop=(kc == KC - 1))
                        fw.tt(ht[:, bb, cg * 512:(cg + 1) * 512], ht[:, bb, cg * 512:(cg + 1) * 512], ps, ALU.add)
                fw.dma('pool', Hout[t0:t0 + 512, :].rearrange("(b p) d -> p b d", p=128), ht)
                it += 1

    def final_phase(Hsrc):
        with Phase(fw, "fin") as ph:
            gbc = rowbc(ph, W['final_norm'], D, "g")
            hts = [ph.sb([128, 4, D], F32, "ht") for _ in range(2)]
            sq = ph.sb([128, D], F32, "sq")
            ss = [ph.sb([128, 4], F32, "ss") for _ in range(2)]
            rs = [ph.sb([128, 4], F32, "rs") for _ in range(2)]
            it = 0
            for t0 in range(0, T, 512):
                ht, ssb, rsb = hts[it % 2], ss[it % 2], rs[it % 2]
                fw.dma('sp', ht, Hsrc[t0:t0 + 512, :].rearrange("(b p) d -> p b d", p=128))
                for bb in range(4):
                    fw.act(sq, ht[:, bb, :], AF.Square, accum=ssb[:, bb:bb + 1])
                rsqrt_col(rsb, ssb, 1.0 / D)
                for bb in range(4):
                    fw.stt(ht[:, bb, :], ht[:, bb, :], rsb[:, bb:bb + 1], gbc, ALU.mult, ALU.mult)
                fw.dma('pool', ytok[t0:t0 + 512, :].rearrange("(b p) d -> p b d", p=128), ht)
                it += 1


    lbt = cst.sb([128, 4, 4], F32, "lbt")
    omlt = cst.sb([128, 4, 4], F32, "omlt")
    nomlt = cst.sb([128, 4, 4], F32, "nomlt")
    if 'D' in feat:
        et = cst.sb([128, 4, 4], F32, "lbe")
        st_ = cst.sb([128, 4], F32, "lbs")
        fw.dma('sp', et, W['hgrn_lb_logits'].rearrange("l (c p) -> p l c", p=128), allow_slow_non_contiguous=True)
        fw.act(et, et, AF.Exp)
        fw.rsum(st_, et.rearrange("p l c -> p c l"))
        fw.op('dve', lambda e: e.reciprocal(out=st_, in_=st_), r=[st_], w=[st_])
        fw.memset(lbt, 0.0)
        for li in range(1, 4):
            fw.tt(et[:, li, :], et[:, li, :], st_, ALU.mult)
            fw.tt(lbt[:, li, :], lbt[:, li - 1, :], et[:, li, :], ALU.add)
        fw.ts(omlt, lbt, -1.0, 1.0, ALU.mult, ALU.add)
        fw.ts(nomlt, lbt, 1.0, -1.0, ALU.mult, ALU.add)

    def mixer_gla(l, mx):
        isB = (mx == 'B')
        base = OFF_B if isB else OFF_D
        CH = 128 if isB else 32
        NCH = 512 // CH
        rst4 = rst128_4 if isB else rst32_4
        for dr in range(2):
            X = XN if dr == 0 else XNR
            with Phase(fw, "g%s%d%d" % (mx, l, dr)) as ph:
                if isB:
                    wq = load_w(ph, W['w_in'][l, :, base:base + 512], D, 512, "wq")
                    wk = load_w(ph, W['w_in'][l, :, base + 512:base + 1024], D, 512, "wk")
                    wv = load_w(ph, W['w_in'][l, :, base + 1024:base + 1536], D, 512, "wv")
                    vb_ap = W['b_in'][l, base + 1024:base + 1536]
                    wo_ap, ob_ap = W['w_in'][l, :, base + 1536:base + 2048], W['b_in'][l, base + 1536:base + 2048]
                    g0c = base + 2048 + dr * 8
                    wgs = load_w(ph, W['w_in'][l, :, g0c:g0c + 8], D, 8, "wgs")
                    gb8 = ph.sb([8, 1], F32, "gb8")
                    fw.dma('sp', gb8, W['b_in'][l, g0c:g0c + 8].rearrange("(p o) -> p o", o=1))
                    gsb = ph.sb([8, 512], F32, "gsb")
                    ghi = [ph.sb([8, 512], BF16, "ghi") for _ in range(2)]
                    glo = [ph.sb([8, 512], BF16, "glo") for _ in range(2)]
                    qb = colload(ph, W['b_in'][l, base:base + 512], 4, "qb")
                    kb = colload(ph, W['b_in'][l, base + 512:base + 1024], 4, "kb")
                else:
                    wq = load_w(ph, W['w_in'][l, :, base:base + 512], D, 512, "wq")
                    wk = load_w(ph, W['w_in'][l, :, base + 512 + dr * 512:base + 1024 + dr * 512], D, 512, "wf")
                    wv = load_w(ph, W['w_in'][l, :, base + 1536:base + 2048], D, 512, "wv")
                    vb_ap = W['b_in'][l, base + 1536:base + 2048]
                    wo_ap, ob_ap = W['w_in'][l, :, base + 2048:base + 2560], W['b_in'][l, base + 2048:base + 2560]
                    qb = colload(ph, W['b_in'][l, base:base + 512], 4, "qb")
                    kb = colload(ph, W['b_in'][l, base + 512 + dr * 512:base + 1024 + dr * 512], 4, "fb")
                vbias = rowbc(ph, vb_ap, 512, "vb")
                if dr == 0:
                    wo = load_w(ph, wo_ap, D, 512, "wo")
                    obias = rowbc(ph, ob_ap, 512, "ob")
                    ogs = [ph.sb([128, 4, 512], F32, "ogs") for _ in range(2)]
                xas = [ph.sb([128, 8, 512], BF16, "xa") for _ in range(2)]
                mk2 = lambda dt_, nm: [ph.sb([128, 2, 512], dt_, nm) for _ in range(2)]
                qraws, kks, css, aas, ees, sgs = (mk2(F32, "qraw"), mk2(F32, "kk"), mk2(F32, "cs"), mk2(F32, "aa"),
                                                  mk2(F32, "ee"), mk2(F32, "sg") if not isB else None)
                qts, kts, khb = mk2(BF16, "qt"), mk2(BF16, "kt"), mk2(BF16, "kh")
                khs = [ph.sb([128, 2, 4, 128], BF16, "khs") for _ in range(2)]
                dls = [ph.sb([128, 2, NCH], F32, "dl") for _ in range(2)]
                vsts = [ph.sb([128, 4, 4, 132], BF16, "vst") for _ in range(2)]
                for v_ in vsts:
                    fw.memset(v_, 0.0)
                    fw.memset(v_[:, :, :, 128:129], 1.0)
                F = lambda t: t.rearrange("p h t -> p (h t)")
                CV = lambda t: t.rearrange("p h (c t) -> p (h c) t", t=CH)
                it = 0
                pi = 0
                ih = 0
                for t0 in range(0, T, 512):
                    xa = xas[it % 2]
                    fw.dma('sp', xa, X[:, t0:t0 + 512].rearrange("(kc p) t -> p kc t", p=128))

                    def proj(wt, col):
                        nonlocal pi
                        ps = PS[pi % 6]
                        pi += 1
                        for kc in range(8):
                            fw.mm(ps, wt(kc, col), xa[:, kc, :], start=(kc == 0), stop=(kc == 7))
                        return ps
                    if isB:
                        pgs = PS[pi % 6]
                        pi += 1
                        for kc in range(8):
                            fw.mm(pgs[0:8, :], wgs[:, kc, :], xa[:, kc, :], start=(kc == 0), stop=(kc == 7))
                        fw.act(gsb, pgs[0:8, :], AF.Identity, bias=gb8)
                        gh, gl = ghi[it % 2], glo[it % 2]
                        fw.copy(gh, gsb)
                        fw.tt(gl, gsb, gh, ALU.subtract)
                    for hb in range(2):
                        s_ = ih % 2
                        ih += 1
                        qraw, kk, cs, aa, ee = qraws[s_], kks[s_], css[s_], aas[s_], ees[s_]
                        qt, kt, kh, khst, dl = qts[s_], kts[s_], khb[s_], khs[s_], dls[s_]
                        for hh_ in range(2):
                            h = 2 * hb + hh_
                            pq = proj(lambda kc, h_: wq[:, kc, h_ * 128:(h_ + 1) * 128], h)
                            if isB:
                                fw.act(qraw[:, hh_, :], pq, AF.Identity, bias=qb[:, h:h + 1])
                            else:
                                fw.act(qraw[:, hh_, :], pq, AF.Silu, bias=qb[:, h:h + 1])
                            pk = proj(lambda kc, h_: wk[:, kc, h_ * 128:(h_ + 1) * 128], h)
                            if isB:
                                fw.ts(kk[:, hh_, :], pk, kb[:, h:h + 1], 128.0 ** -0.5, ALU.add, ALU.mult)
                                pig = PS[pi % 6]
                                pi += 1
                                fw.mm(pig, sel_b[:, h, :], gh, start=True, stop=False)
                                fw.mm(pig, sel_b[:, h, :], gl, start=False, stop=True)
                                fw.copy(aa[:, hh_, :], pig, 'act')
                                pfg = PS[pi % 6]
                                pi += 1
                                fw.mm(pfg, sel_b[:, 4 + h, :], gh, start=True, stop=False)
                                fw.mm(pfg, sel_b[:, 4 + h, :], gl, start=False, stop=True)
                                fw.act(ee[:, hh_, :], pfg, AF.Exp, scale=-1.0)
                            else:
                                sg = sgs[s_]
                                fw.act(sg[:, hh_, :], pk, AF.Sigmoid, bias=kb[:, h:h + 1])
                                fw.ts(ee[:, hh_, :], sg[:, hh_, :], omlt[:, l, h:h + 1], lbt[:, l, h:h + 1], ALU.mult, ALU.add)
                                fw.ts(kk[:, hh_, :], sg[:, hh_, :], nomlt[:, l, h:h + 1], omlt[:, l, h:h + 1], ALU.mult, ALU.add,
                                      eng='pool')
                        r4 = rst4[:, 0:1024]
                        if isB:
                            fw.act(F(ee), F(ee), AF.Ln, bias=1.0)
                            fw.scan(F(cs), r4, F(ee), 0.0, ALU.mult, ALU.subtract)
                            fw.tt(F(aa), F(aa), F(cs), ALU.subtract)
                        else:
                            fw.act(F(ee), F(ee), AF.Ln)
                            fw.scan(F(cs), r4, F(ee), 0.0, ALU.mult, ALU.add)
                            fw.ts(F(aa), F(cs), -1.0, None, ALU.mult, eng='pool')
                        fw.act(F(ee), F(cs), AF.Exp)
                        fw.tt(F(qt), F(qraw), F(ee), ALU.mult)
                        fw.act(F(ee), F(aa), AF.Exp)
                        fw.tt(F(kt), F(kk), F(ee), ALU.mult)
                        fw.tt(CV(aa), CV(aa), CV(cs)[:, :, CH - 1:CH].to_broadcast([128, 2 * NCH, CH]), ALU.add)
                        fw.act(F(aa), F(aa), AF.Exp)
                        fw.tt(F(kh), F(kk), F(aa), ALU.mult, eng='pool')
                        fw.act(dl.rearrange("p h c -> p (h c)"), CV(cs)[:, :, CH - 1], AF.Exp)
                        for hh_ in range(2):
                            pt_ = PS[6 + hh_ % 2]
                            for bb in range(4):
                                fw.mm(pt_[:, bb * 128:(bb + 1) * 128], kh[:, hh_, bb * 128:(bb + 1) * 128], ident_b)
                            fw.copy(khst[:, hh_, :, :].rearrange("p b d -> p (b d)"), pt_, 'act')
                        h2 = slice(2 * hb, 2 * hb + 2)
                        fw.dma('pool', QT[dr, h2, :, t0:t0 + 512].rearrange("h p t -> p h t"), qt)
                        fw.dma('pool', KT[dr, h2, :, t0:t0 + 512].rearrange("h p t -> p h t"), kt)
                        for hh_ in range(2):
                            fw.dma('pool', KH[dr, 2 * hb + hh_, t0:t0 + 512, :].rearrange("(b p) d -> p b d", p=128),
                                   khst[:, hh_, :, :])
                        fw.dma('pool', DL[dr, h2, :, t0 // CH:t0 // CH + NCH].rearrange("h p c -> p h c"), dl)
                    vst = vsts[it % 2]
                    for bb in range(4):
                        pv = PS[6 + bb % 2]
                        for kc in range(8):
                            fw.mm(pv, xa[:, kc, bb * 128:(bb + 1) * 128], wv[:, kc, :], start=(kc == 0), stop=(kc == 7))
                        fw.tt(vst[:, bb, :, 0:128], pv.rearrange("p (h c) -> p h c", c=128),
                              vbias.rearrange("p (h c) -> p h c", c=128), ALU.add)
                    fw.dma('pool', VV[dr, t0:t0 + 512, :, :].rearrange("(b p) h c -> p b h c", p=128), vst)
                    if dr == 0:
                        og = ogs[it % 2]
                        for bb in range(4):
                            po = PS[6 + bb % 2]
                            for kc in range(8):
                                fw.mm(po, xa[:, kc, bb * 128:(bb + 1) * 128], wo[:, kc, :], start=(kc == 0), stop=(kc == 7))
                            fw.tt(og[:, bb, :], po, obias, ALU.add)
                            fw.act(og[:, bb, :], og[:, bb, :], AF.Sigmoid)
                        fw.dma('pool', OG[t0:t0 + 512, :].rearrange("(b p) c -> p b c", p=128), og)
                    it += 1
        if int(os.environ.get('GLA_STAGE', '9')) < 2:
            return
        NSUB = 128 // CH
        with Phase(fw, "c%s%d" % (mx, l)) as ph:
            S32 = [[ph.sb([128, 132], F32, "S") for _ in range(4)] for _ in range(2)]
            RB = 2 * NSUB + 2
            Sb = [[[ph.sb([128, 132], BF16, "Sb") for _ in range(RB)] for _ in range(4)] for _ in range(2)]
            qTs = [ph.sb([128, 512], BF16, "q") for _ in range(3)]
            kTs = [ph.sb([128, 512], BF16, "k") for _ in range(3)]
            khs = [ph.sb([128, 4, 128], BF16, "kh") for _ in range(3)]
            vts = [ph.sb([128, 4, 132], BF16, "v") for _ in range(3)]
            dls = [ph.sb([128, 4 * NSUB], F32, "dl") for _ in range(3)]
            atms = [ph.sb([128, 128], BF16, "atm") for _ in range(2)]
            osts = [ph.sb([128, 4, 132], F32, "ost") for _ in range(2)]
            qzs = [ph.sb([128, 4, NSUB, 128], BF16, "qz") for _ in range(2)]
            khzs = [ph.sb([128, 4, NSUB, 128], BF16, "khz") for _ in range(2)]
            mk = mask128_f if isB else mask_f
            it = 0
            ib = 0
            for si, L in enumerate(seqs):
                for dr in range(2):
                    for h in range(4):
                        fw.memset(S32[dr][h], 0.0)
                        fw.memset(Sb[dr][h][0], 0.0)
                cstep = [[0] * 4 for _ in range(2)]
                for t0 in range(0, L, 512):
                    g0 = soff[si] + t0
                    for dr in range(2):
                        for h in range(4):
                            qT, kT, khst, vt, dl, ost = qTs[it % 3], kTs[it % 3], khs[it % 3], vts[it % 3], dls[it % 3], osts[it % 2]
                            qz, khz = qzs[it % 2], khzs[it % 2]
                            it += 1
                            fw.dma('sp', qT, QT[dr, h, :, g0:g0 + 512])
                            fw.dma('sp', kT, KT[dr, h, :, g0:g0 + 512])
                            fw.dma('sp', khst, KH[dr, h, g0:g0 + 512, :].rearrange("(b p) d -> p b d", p=128))
                            fw.dma('sp', vt, VV[dr, g0:g0 + 512, h, :].rearrange("(b p) c -> p b c", p=128))
                            fw.dma('sp', dl, DL[dr, h, :, g0 // CH:g0 // CH + 4 * NSUB])
                            S = S32[dr][h]
                            qv = qT.rearrange("p (b j) -> p b j", j=128)
                            if NSUB > 1:
                                fw.tt(qz if dr == 0 else qz[:, :, :, ::-1],
                                      qv.unsqueeze(2).to_broadcast([128, 4, NSUB, 128]),
                                      cmask_b.unsqueeze(1).to_broadcast([128, 4, NSUB, 128]), ALU.mult)
                                fw.tt(khz, khst.unsqueeze(2).to_broadcast([128, 4, NSUB, 128]),
                                      rmask_f.unsqueeze(1).unsqueeze(3).to_broadcast([128, 4, NSUB, 128]), ALU.mult,
                                      eng='pool')
                            elif dr == 1:
                                fw.copy(qz[:, :, 0, ::-1], qv, 'pool')
                            units = []

                            def stage1(bb):
                                nonlocal ib
                                pat = PS[ib % 2]
                                po = PS[2 + ib % 2]
                                pd = (PS[4 + 2 * (ib % 2)], PS[5 + 2 * (ib % 2)])
                                atm = atms[ib % 2]
                                ib += 1
                                sl = slice(bb * 128, (bb + 1) * 128)
                                fw.mm(pat[:, 0:128], kT[:, sl], qT[:, sl])
                                dsl = []
                                for c in range(NSUB):
                                    dd = pd[c // 2][:, (c % 2) * 256:(c % 2) * 256 + 132]
                                    fw.mm(dd, khz[:, bb, c, :] if NSUB > 1 else khst[:, bb, :], vt[:, bb, :])
                                    dsl.append(dd)
                                fw.tt(atm if dr == 0 else atm[:, ::-1], pat[:, 0:128], mk, ALU.mult)
                                sbs = []
                                for c in range(NSUB):
                                    cidx = bb * NSUB + c
                                    cs_ = cstep[dr][h]
                                    cstep[dr][h] += 1
                                    sbs.append(Sb[dr][h][cs_ % RB])
                                    fw.stt(S, S, dl[:, cidx:cidx + 1], dsl[c], ALU.mult, ALU.add)
                                    fw.copy(Sb[dr][h][(cs_ + 1) % RB], S, 'act')
                                return (bb, po, atm, sbs, sl)

                            def stage2(u):
                                bb, po, atm, sbs, sl = u
                                fw.mm(po[:, 0:132], atm, vt[:, bb, :], start=True, stop=False)
                                for c in range(NSUB):
                                    if NSUB > 1 or dr == 1:
                                        lq = qz[:, bb, c, :]
                                    else:
                                        lq = qT[:, sl]
                                    fw.mm(po[:, 0:132], lq, sbs[c], start=False, stop=(c == NSUB - 1))
                                fw.copy(ost[:, bb if dr == 0 else 3 - bb, :], po[:, 0:132], 'act')
                            prev = stage1(0)
                            for bb in range(4):
                                nxt_ = stage1(bb + 1) if bb < 3 else None
                                stage2(prev)
                                prev = nxt_
                            go = g0 if dr == 0 else soff[si] + L - 512 - t0
                            fw.dma('pool', OO[dr, go:go + 512, h, :].rearrange("(b p) c -> p b c", p=128), ost)
        if int(os.environ.get('GLA_STAGE', '9')) < 3:
            return
        with Phase(fw, "m%s%d" % (mx, l)) as ph:
            gain = rowbc(ph, W['mlstm_norm' if isB else 'hgrn_norm'][l], 512, "gn")
            ofs = [ph.sb([128, 4, 4, 132], F32, "of") for _ in range(2)]
            obv = [ph.sb([128, 4, 4, 132], F32, "ob") for _ in range(2)]
            ogt = [ph.sb([128, 4, 512], F32, "og") for _ in range(2)]
            obr = ph.sb([128, 4, 132], F32, "obr")
            hs = ph.sb([128, 4, 128], F32, "hs")
            hb = ph.sb([128, 4, 128], F32, "hb")
            sq = ph.sb([128, 4, 128], F32, "sq")
            rf = ph.sb([128, 4, 1], F32, "rf")
            rb = ph.sb([128, 4, 1], F32, "rb")
            ss = ph.sb([128, 4], F32, "ss")
            rs = ph.sb([128, 4], F32, "rs")
            hsb = ph.sb([128, 512], BF16, "hsb")
            stg = [ph.sb([128, 4, 512], BF16, "stg") for _ in range(2)]
            it = 0
            for si, L in enumerate(seqs):
                for t0 in range(0, L, 512):
                    g0 = soff[si] + t0
                    of, ob_, og, st = ofs[it % 2], obv[it % 2], ogt[it % 2], stg[it % 2]
                    fw.dma('sp', of, OO[0, g0:g0 + 512, :, :].rearrange("(b p) h c -> p b h c", p=128))
                    fw.dma('sp', ob_, OO[1, g0:g0 + 512, :, :].rearrange("(b p) h c -> p b h c", p=128))
                    fw.dma('sp', og, OG[g0:g0 + 512, :].rearrange("(b p) c -> p b c", p=128))
                    for bb in range(4):
                        obr = ob_[:, bb, :, :]
                        if isB:
                            fw.act(rf, of[:, bb, :, 128:129], AF.Abs)
                            fw.ts(rf, rf, 1.0, None, ALU.max)
                            fw.op('dve', lambda e: e.reciprocal(out=rf, in_=rf), r=[rf], w=[rf])
                            fw.act(rb, obr[:, :, 128:129], AF.Abs)
                            fw.ts(rb, rb, 1.0, None, ALU.max)
                            fw.op('dve', lambda e: e.reciprocal(out=rb, in_=rb), r=[rb], w=[rb])
                            fw.tt(hs, of[:, bb, :, 0:128], rf.to_broadcast([128, 4, 128]), ALU.mult)
                            fw.tt(hb, obr[:, :, 0:128], rb.to_broadcast([128, 4, 128]), ALU.mult)
                            fw.tt(hs, hs, hb, ALU.add)
                        else:
                            fw.tt(hs, of[:, bb, :, 0:128], obr[:, :, 0:128], ALU.add)
                        fw.tt(sq, hs, hs, ALU.mult)
                        fw.rsum(ss, sq)
                        rsqrt_col(rs, ss, 1.0 / 128)
                        fw.tt(hs, hs, rs.unsqueeze(2).to_broadcast([128, 4, 128]), ALU.mult)
                        fw.tt(hs, hs, gain.rearrange("p (h c) -> p h c", c=128), ALU.mult)
                        fw.tt(hsb.rearrange("p (h c) -> p h c", c=128), hs, og[:, bb, :].rearrange("p (h c) -> p h c", c=128), ALU.mult)
                        pt_ = PS[4 + bb % 2]
                        for h in range(4):
                            fw.mm(pt_[:, h * 128:(h + 1) * 128], hsb[:, h * 128:(h + 1) * 128], ident_b)
                        fw.copy(st[:, :, bb * 128:(bb + 1) * 128], pt_.rearrange("p (h t) -> p h t", t=128), 'act')
                    fw.dma('pool', BR[1 if isB else 3][:, g0:g0 + 512].rearrange("(h p) t -> p h t", p=128), st)
                    it += 1


    def gen_tables(li, L):
        NCn = L // 128
        M1 = (4 * L) // 64
        with Phase(fw, "tg%d" % li) as ph:
            fx = ph.sb([128, 2, NCn], F32, "fx")
            ix = ph.sb([128, 2, NCn], F32, "ix")
            fw.dma('sp', fx, c_fx[li].rearrange("a p n -> p a n"))
            fw.dma('sp', ix, c_ix[li].rearrange("a p n -> p a n"))
            r0 = ph.sb([128, 128], F32, "r0")
            t0r = ph.sb([128, 512], F32, "t0r")
            fw.dma('sp', r0, c_r0)
            fw.dma('sp', t0r, c_t0)
            yrow = ph.sb([128, 512], F32, "yrow")
            A = ph.sb([128, 2048], F32, "A")
            Ai = ph.sb([128, 2048], I32, "Ai")
            Bq = ph.sb([128, 2048], F32, "Bq")
            mi = ph.sb([128, 2048], I32, "mi")
            mc = ph.sb([128, 2048], I32, "mc")
            ms = ph.sb([128, 2048], I32, "ms")
            snb = [ph.sb([128, 2048], BF16, "sn") for _ in range(2)]
            csb = [ph.sb([128, 2048], BF16, "cs") for _ in range(2)]
            cnt = [0]

            def gen(dC, dS, xhi, xlo, yr, NA, NJ):
                n = NA * NJ
                v = lambda t: t[:, 0:n].rearrange("p (a j) -> p a j", j=NJ)
                xb = lambda x: x.unsqueeze(2).to_broadcast([128, NA, NJ])
                yb = yr.unsqueeze(1).to_broadcast([128, NA, NJ])
                sn, cs_ = snb[cnt[0] % 2], csb[cnt[0] % 2]
                cnt[0] += 1
                fw.tt(v(Ai), yb, xb(xhi), ALU.mult)
                fw.tss(Ai[:, 0:n], Ai[:, 0:n], M1 - 1, ALU.bitwise_and)
                fw.tt(v(Bq), yb, xb(xlo), ALU.mult)
                fw.stt(mi[:, 0:n], Ai[:, 0:n], 64.0, Bq[:, 0:n], ALU.mult, ALU.add)
                fw.tss(mc[:, 0:n], mi[:, 0:n], L, ALU.add)
                fw.tss(mc[:, 0:n], mc[:, 0:n], 4 * L - 1, ALU.bitwise_and)
                fw.tss(ms[:, 0:n], mi[:, 0:n], 4 * L - 1, ALU.bitwise_and)
                fw.act(sn[:, 0:n], ms[:, 0:n], AF.Sin, bias=PI_S, scale=-math.pi / (2 * L))
                fw.act(cs_[:, 0:n], mc[:, 0:n], AF.Sin, bias=PI_S, scale=-math.pi / (2 * L))
                fw.dma('pool', dS, v(sn))
                fw.dma('pool', dC, v(cs_))

            NA = min(16, NCn)
            for kc in range(NCn):
                fw.ts(yrow[:, 0:128], r0, float(256 * kc), None, ALU.add)
                for sub in range(NCn // NA):
                    a = slice(sub * NA, (sub + 1) * NA)
                    gen(TBFC[li][kc, :, a, :], TBFS[li][kc, :, a, :], fx[:, 0, a], fx[:, 1, a], yrow[:, 0:128], NA, 128)
            NA = min(4, NCn)
            for tt in range(L // 512):
                fw.ts(yrow, t0r, float(512 * tt), None, ALU.add)
                for sub in range(NCn // NA):
                    a = slice(sub * NA, (sub + 1) * NA)
                    gen(TBIC[li][tt, :, a, :], TBIS[li][tt, :, a, :], ix[:, 0, a], ix[:, 1, a], yrow, NA, 512)

    def col64(ph, vec_ap, name):
        t = ph.sb([64, 1], F32, name)
        fw.dma('sp', t, vec_ap.rearrange("(j p) -> p j", p=64), allow_slow_non_contiguous=True)
        return t

    def hilo(ph, src, shape, name):
        hi = ph.sb(shape, BF16, name + "h")
        lo = ph.sb(shape, BF16, name + "l")
        fw.copy(hi, src)
        fw.tt(lo, src, hi, ALU.subtract)
        return hi, lo

    def hyena_filter(l, li, L):
        with Phase(fw, "hf%d_%d" % (l, li)) as ph:
            w1 = ph.sb([33, 64], F32, "w1")
            w2 = ph.sb([64, 64], F32, "w2")
            fw.dma('sp', w1, W['hyena_w1'][l])
            fw.dma('sp', w2, W['hyena_w2'][l])
            w1h, w1l = hilo(ph, w1, [33, 64], "w1")
            w2h, w2l = hilo(ph, w2, [64, 64], "w2")
            w3b = ph.sb([64, 1024], BF16, "w3")
            fw.dma('pool', w3b, W['hyena_w3'][l])
            b1 = col64(ph, W['hyena_b1'][l], "b1")
            b2 = col64(ph, W['hyena_b2'][l], "b2")
            f1 = col64(ph, W['hyena_freq1'][l], "f1")
            f2 = col64(ph, W['hyena_freq2'][l], "f2")
            fw.ts(f1, f1, 1.0 / (2 * math.pi), None, ALU.mult)
            fw.ts(f2, f2, 1.0 / (2 * math.pi), None, ALU.mult)
            ratebc = rowbc(ph, c_rate, 512, "rate")
            ntc = colload(ph, c_nt[li], L // 128, "nt")
            ft = ph.sb([33, 512], F32, "ft")
            fth = ph.sb([33, 512], BF16, "fth")
            ftl = ph.sb([33, 512], BF16, "ftl")
            q = ph.sb([64, 512], F32, "q")
            qi = ph.sb([64, 512], I32, "qi")
            hd = ph.sb([64, 512], F32, "hd")
            hdh = ph.sb([64, 512], BF16, "hdh")
            hdl = ph.sb([64, 512], BF16, "hdl")
            dec = ph.sb([128, 512], F32, "dec")
            h0 = ph.sb([128, 512], F32, "h0")
            h1 = ph.sb([128, 512], F32, "h1")
            stg = [ph.sb([128, 4, 1024], BF16, "stg") for _ in range(2)]

            def sin_layer(ps, bcol, fcol, dst):
                fw.ts(q, ps, bcol, fcol, ALU.add, ALU.mult)
                fw.copy(qi, q)
                fw.tt(q, q, qi, ALU.subtract)
                fw.act(dst, q, AF.Sin, scale=6.283185)

            it = 0
            for t0 in range(0, L, 512):
                fw.dma('sp', ft, c_feat[li][:, t0:t0 + 512])
                fw.copy(fth, ft)
                fw.tt(ftl, ft, fth, ALU.subtract)
                ps = PS[0][0:64, :]
                fw.mm(ps, w1h, fth, start=True, stop=False)
                fw.mm(ps, w1h, ftl, start=False, stop=False)
                fw.mm(ps, w1l, fth, start=False, stop=True)
                sin_layer(ps, b1, f1, hd)
                fw.copy(hdh, hd)
                fw.tt(hdl, hd, hdh, ALU.subtract)
                ps2 = PS[1][0:64, :]
                fw.mm(ps2, w2h, hdh, start=True, stop=False)
                fw.mm(ps2, w2h, hdl, start=False, stop=False)
                fw.mm(ps2, w2l, hdh, start=False, stop=True)
                sin_layer(ps2, b2, f2, hd)
                fw.copy(hdh, hd)
                st = stg[it % 2]
                for bb in range(4):
                    blk = t0 // 128 + bb
                    p0, p1 = PS[2 + 2 * (bb % 2)], PS[3 + 2 * (bb % 2)]
                    fw.mm(p0, hdh[:, bb * 128:(bb + 1) * 128], w3b[:, 0:512])
                    fw.mm(p1, hdh[:, bb * 128:(bb + 1) * 128], w3b[:, 512:1024])
                    fw.act(dec, ratebc, AF.Exp, scale=ntc[:, blk:blk + 1])
                    fw.tt(h0, p0, dec, ALU.mult)
                    fw.tt(h1, p1, dec, ALU.mult)
                    fw.tt(st[:, bb, 0:512], h0, h1, ALU.add)
                    fw.tt(st[:, bb, 512:1024], h0, h1, ALU.subtract)
                fw.dma('pool', FH[li][t0:t0 + 512, :].rearrange("(b p) c -> p b c", p=128), st)
                it += 1

    def mixer_c(l):
        for li, L in enumerate(Ls):
            hyena_filter(l, li, L)
        with Phase(fw, "pc%d" % l) as ph:
            wsb = load_w(ph, W['w_in'][l, :, OFF_C:OFF_C + 1536], D, 1536, "w")
            bcol = colload(ph, W['b_in'][l, OFF_C:OFF_C + 1536], 12, "b")
            stg = [ph.sb([128, 12, 512], F32, "stg") for _ in range(2)]

            def epi(oc, ps, t0, it):
                fw.act(stg[it % 2][:, oc, :], ps, AF.Identity, bias=bcol[:, oc:oc + 1])
                if oc == 11:
                    fw.dma('pool', PC[:, t0:t0 + 512].rearrange("(c p) t -> p c t", p=128), stg[it % 2])
            linear_fm(ph, XN, D, wsb, 1536, epi, "lc")
        with Phase(fw, "hz%d" % l) as ph:
            cw = ph.sb([128, 3, 12], F32, "cw")
            fw.dma('sp', cw, W['hyena_conv_w'][l].rearrange("j (c p) -> p j c", p=128), allow_slow_non_contiguous=True)
            cb = colload(ph, W['hyena_conv_b'][l], 12, "cb")
            xps = [ph.sb([128, 514], F32, "xp") for _ in range(3)]
            cvs = [[ph.sb([128, 512], F32, "cv") for _ in range(3)] for _ in range(2)]
            zts = [ph.sb([128, 512], F32, "z") for _ in range(2)]
            zb = ph.sb([128, 512], BF16, "zb")
            ztt = [ph.sb([128, 4, 128], BF16, "ztt") for _ in range(2)]
            it = 0
            for si, L in enumerate(seqs):
                for t0 in range(0, L, 512):
                    g0 = soff[si] + t0
                    lo = 1 if t0 == 0 else 0
                    hi = 1 if t0 + 512 == L else 0
                    for cc in range(4):
                        cv, zt = cvs[it % 2], zts[it % 2]
                        for qi_ in range(3):
                            xp = xps[qi_]
                            col = qi_ * 4 + cc
                            if lo or hi:
                                fw.memset(xp, 0.0)
                            fw.dma('sp', xp[:, lo:514 - hi], PC[col * 128:(col + 1) * 128, g0 - 1 + lo:g0 + 513 - hi])
                            fw.act(cv[qi_], xp[:, 1:513], AF.Identity, bias=cb[:, col:col + 1], scale=cw[:, 1, col:col + 1])
                            fw.stt(cv[qi_], xp[:, 0:512], cw[:, 0, col:col + 1], cv[qi_], ALU.mult, ALU.add)
                            fw.stt(cv[qi_], xp[:, 2:514], cw[:, 2, col:col + 1], cv[qi_], ALU.mult, ALU.add)
                        fw.tt(zt, cv[2], cv[0], ALU.mult)
                        fw.copy(zb, zt, 'act')
                        fw.dma('pool', ZZ[cc * 128:(cc + 1) * 128, g0:g0 + 512], zt)
                        fw.dma('pool', X1[cc * 128:(cc + 1) * 128, g0:g0 + 512], cv[1])
                        pt_ = PS[it % 2]
                        for bb in range(4):
                            fw.mm(pt_[:, bb * 128:(bb + 1) * 128], zb[:, bb * 128:(bb + 1) * 128], ident_b)
                        zt2 = ztt[it % 2]
                        fw.copy(zt2.rearrange("p b c -> p (b c)"), pt_, 'act')
                        fw.dma('pool', ZT[g0:g0 + 512, cc * 128:(cc + 1) * 128].rearrange("(b p) c -> p b c", p=128), zt2)
                        it += 1
        for li, L in enumerate(Ls):
            NCn = L // 128
            sis = [i for i, Lx in enumerate(seqs) if Lx == L]
            with Phase(fw, "hg%d_%d" % (l, li)) as ph:
                mov = ph.sb([128, NCn, 512], BF16, "mov")
                tbs = [ph.sb([128, NCn, 128], BF16, "tb") for _ in range(2)]
                gst = [ph.sb([128, 512], F32, "g") for _ in range(2)]
                it = 0
                for (tb_d, c0, dst) in ((TBFC[li], 0, GR[li]), (TBFS[li], 512, GS[li])):
                    fw.dma('sp', mov, FH[li][:, c0:c0 + 512].rearrange("(n p) c -> p n c", p=128))
                    for kc in range(NCn):
                        tb = tbs[it % 2]
                        fw.dma('sp', tb, tb_d[kc])
                        ps = PS[it % 4]
                        for n_ in range(NCn):
                            fw.mm(ps, tb[:, n_, :], mov[:, n_, :], start=(n_ == 0), stop=(n_ == NCn - 1))
                        fw.act(gst[it % 2], ps, AF.Copy, scale=1.0 / L)
                        fw.dma('pool', dst[kc * 128:(kc + 1) * 128, :], gst[it % 2])
                        it += 1
            with Phase(fw, "hx%d_%d" % (l, li)) as ph:
                zr = [ph.sb([128, NCn, 512], BF16, "zr") for _ in sis]
                tbc = [ph.sb([128, NCn, 128], BF16, "tbc") for _ in range(2)]
                tbsn = [ph.sb([128, NCn, 128], BF16, "tbs") for _ in range(2)]
                grt = [ph.sb([128, 512], F32, "gr") for _ in range(2)]
                gs_t = [ph.sb([128, 512], F32, "gs") for _ in range(2)]
                t1 = ph.sb([128, 512], F32, "t1")
                t2 = ph.sb([128, 512], F32, "t2")
                yrt = [ph.sb([128, 512], BF16, "yr") for _ in range(2)]
                yst = [ph.sb([128, 512], BF16, "ys") for _ in range(2)]
                for j, si in enumerate(sis):
                    fw.dma('sp', zr[j], ZT[soff[si]:soff[si] + L, :].rearrange("(n p) c -> p n c", p=128))
                it = 0
                for kc in range(NCn):
                    tc_, ts_ = tbc[kc % 2], tbsn[kc % 2]
                    fw.dma('sp', tc_, TBFC[li][kc])
                    fw.dma('sp', ts_, TBFS[li][kc])
                    gr, gs = grt[kc % 2], gs_t[kc % 2]
                    fw.dma('sp', gr, GR[li][kc * 128:(kc + 1) * 128, :])
                    fw.dma('sp', gs, GS[li][kc * 128:(kc + 1) * 128, :])
                    for j, si in enumerate(sis):
                        pr, ps_ = PS[(2 * it) % 8], PS[(2 * it + 1) % 8]
                        for n_ in range(NCn):
                            fw.mm(pr, tc_[:, n_, :], zr[j][:, n_, :], start=(n_ == 0), stop=(n_ == NCn - 1))
                        for n_ in range(NCn):
                            fw.mm(ps_, ts_[:, n_, :], zr[j][:, n_, :], start=(n_ == 0), stop=(n_ == NCn - 1))
                        yr, ys = yrt[it % 2], yst[it % 2]
                        fw.tt(t1, pr, gr, ALU.mult)
                        fw.tt(t2, ps_, gs, ALU.mult)
                        fw.tt(yr, t1, t2, ALU.subtract)
                        fw.tt(t1, pr, gs, ALU.mult)
                        fw.tt(t2, ps_, gr, ALU.mult)
                        fw.tt(ys, t1, t2, ALU.add)
                        r0_ = soff[si] + kc * 128
                        fw.dma('pool', YR[r0_:r0_ + 128, :], yr)
                        fw.dma('pool', YS[r0_:r0_ + 128, :], ys)
                        it += 1
            with Phase(fw, "hi%d_%d" % (l, li)) as ph:
                skipc = colload(ph, W['hyena_skip'][l], 4, "skip")
                yrr = ph.sb([128, NCn, 512], BF16, "yrr")
                ysr = ph.sb([128, NCn, 512], BF16, "ysr")
                KS = min(8, NCn)
                tcs = [ph.sb([128, KS, 512], BF16, "tc") for _ in range(2)]
                tss_ = [ph.sb([128, KS, 512], BF16, "ts") for _ in range(2)]
                zt = [ph.sb([128, 512], F32, "z") for _ in range(2)]
                x1t = [ph.sb([128, 512], F32, "x1") for _ in range(2)]
                ob = [ph.sb([128, 512], BF16, "ob") for _ in range(2)]
                it = 0
                ie = 0
                for si in sis:
                    fw.dma('sp', yrr, YR[soff[si]:soff[si] + L, :].rearrange("(n p) c -> p n c", p=128))
                    fw.dma('sp', ysr, YS[soff[si]:soff[si] + L, :].rearrange("(n p) c -> p n c", p=128))
                    for tt in range(L // 512):
                        g0 = soff[si] + tt * 512
                        pacc = [PS[4 * (tt % 2) + cc] for cc in range(4)]
                        for sub in range(NCn // KS):
                            tc_, ts_ = tcs[it % 2], tss_[it % 2]
                            it += 1
                            a = slice(sub * KS, (sub + 1) * KS)
                            fw.dma('sp', tc_, TBIC[li][tt, :, a, :])
                            fw.dma('sp', ts_, TBIS[li][tt, :, a, :])
                            for k_ in range(KS):
                                kc = sub * KS + k_
                                for cc in range(4):
                                    fw.mm(pacc[cc], yrr[:, kc, cc * 128:(cc + 1) * 128], tc_[:, k_, :],
                                          start=(kc == 0), stop=False)
                                    fw.mm(pacc[cc], ysr[:, kc, cc * 128:(cc + 1) * 128], ts_[:, k_, :],
                                          start=False, stop=(kc == NCn - 1))
                        for cc in range(4):
                            z_, x_, o_ = zt[ie % 2], x1t[ie % 2], ob[ie % 2]
                            ie += 1
                            fw.dma('sp', z_, ZZ[cc * 128:(cc + 1) * 128, g0:g0 + 512])
                            fw.dma('sp', x_, X1[cc * 128:(cc + 1) * 128, g0:g0 + 512])
                            fw.stt(z_, z_, skipc[:, cc:cc + 1], pacc[cc], ALU.mult, ALU.add)
                            fw.tt(o_, z_, x_, ALU.mult)
                            fw.dma('pool', BR[2][cc * 128:(cc + 1) * 128, g0:g0 + 512], o_)

    def ffn_phase(l, Hin, Hout):
        with Phase(fw, "ff%d" % l) as ph:
            wu = load_w(ph, W['w_up'][l], D, 2 * DFF, "wu")
            wd = load_w(ph, W['w_down'][l], DFF, D, "wd")
            cw = ph.sb([128, 3, 32], F32, "cw")
            fw.dma('sp', cw, W['ffn_conv_w'][l].rearrange("j (c p) -> p j c", p=128), allow_slow_non_contiguous=True)
            cb = colload(ph, W['ffn_conv_b'][l], 32, "cb")
            xas = [ph.sb([128, 8, 512], BF16, "xa") for _ in range(2)]
            hts = [ph.sb([128, 4, D], F32, "ht") for _ in range(2)]
            ca = ph.sb([128, 510], F32, "ca")
            cbt = ph.sb([128, 510], F32, "cbt")
            act_t = [ph.sb([128, 16, 512], BF16, "act") for _ in range(2)]
            it = 0
            pi = 0
            for si, L in enumerate(seqs):
                for t0 in range(0, L, 510):
                    nout = min(510, L - t0)
                    g0 = soff[si] + t0
                    xa, ht, at = xas[it % 2], hts[it % 2], act_t[it % 2]
                    lo = 1 if t0 == 0 else 0
                    ncol = min(512, L - (t0 - 1)) - lo
                    if lo or (lo + ncol) < 512:
                        fw.memset(xa, 0.0)
                    fw.dma('sp', xa[:, :, lo:lo + ncol],
                           XN[:, g0 - 1 + lo:g0 - 1 + lo + ncol].rearrange("(kc p) t -> p kc t", p=128))
                    nb = (nout + 127) // 128
                    for bb in range(nb):
                        r = min(128, nout - bb * 128)
                        fw.dma('sp', ht[0:r, bb, :], Hin[g0 + bb * 128:g0 + bb * 128 + r, :])
                    for oc in range(16):
                        for half, dst in ((0, ca), (1, cbt)):
                            col = oc + 16 * half
                            ps = PS[pi % 4]
                            pi += 1
                            for kc in range(8):
                                fw.mm(ps, wu[:, kc, col * 128:(col + 1) * 128], xa[:, kc, :], start=(kc == 0), stop=(kc == 7))
                            fw.act(dst, ps[:, 1:511], AF.Identity, bias=cb[:, col:col + 1], scale=cw[:, 1, col:col + 1])
                            fw.stt(dst, ps[:, 0:510], cw[:, 0, col:col + 1], dst, ALU.mult, ALU.add)
                            fw.stt(dst, ps[:, 2:512], cw[:, 2, col:col + 1], dst, ALU.mult, ALU.add)
                        fw.act(ca, ca, AF.Gelu_apprx_tanh)
                        fw.tt(at[:, oc, 0:510], ca, cbt, ALU.mult)
                    for bb in range(nb):
                        r = min(128, nout - bb * 128)
                        for cg in range(2):
                            ps = PS[4 + pi % 4]
                            pi += 1
                            for kc in range(16):
                                fw.mm(ps[0:r, :], at[:, kc, bb * 128:bb * 128 + r], wd[:, kc, cg * 512:(cg + 1) * 512],
                                      start=(kc == 0), stop=(kc == 15))
                            fw.tt(ht[0:r, bb, cg * 512:(cg + 1) * 512], ht[0:r, bb, cg * 512:(cg + 1) * 512], ps[0:r, :], ALU.add)
                        fw.dma('pool', Hout[g0 + bb * 128:g0 + bb * 128 + r, :], ht[0:r, bb, :])
                    it += 1

    def ple_phase(l, Hin, Hout):
        with Phase(fw, "pl%d" % l) as ph:
            wg = load_w(ph, W['w_ple_gate'][l], D, D, "wg")
            wp = load_w(ph, W['w_ple'][l], DPLE, D, "wp")
            xas = [ph.sb([128, 8, 512], BF16, "xa") for _ in range(2)]
            hts = [ph.sb([128, 4, D], F32, "ht") for _ in range(2)]
            pts = [ph.sb([128, 4, DPLE], F32, "pt") for _ in range(2)]
            pb = ph.sb([128, 4, DPLE], BF16, "pb")
            pT = ph.sb([128, 2, 512], BF16, "pT")
            gs = [ph.sb([128, 512], F32, "gs") for _ in range(2)]
            it = 0
            pi = 0
            for t0 in range(0, T, 512):
                xa, ht, pt = xas[it % 2], hts[it % 2], pts[it % 2]
                fw.dma('sp', xa, XN[:, t0:t0 + 512].rearrange("(kc p) t -> p kc t", p=128))
                fw.dma('sp', ht, Hin[t0:t0 + 512, :].rearrange("(b p) d -> p b d", p=128))
                fw.dma('sp', pt, ptok[l, t0:t0 + 512, :].rearrange("(b p) d -> p b d", p=128))
                fw.copy(pb, pt, 'act')
                for kc2 in range(2):
                    ps = PS[pi % 8]
                    pi += 1
                    for bb in range(4):
                        fw.mm(ps[:, bb * 128:(bb + 1) * 128], pb[:, bb, kc2 * 128:(kc2 + 1) * 128], ident_b)
                    fw.copy(pT[:, kc2, :], ps)
                for bb in range(4):
                    for cg in range(2):
                        pg, pp = PS[pi % 8], PS[(pi + 1) % 8]
                        pi += 2
                        for kc in range(8):
                            fw.mm(pg, xa[:, kc, bb * 128:(bb + 1) * 128], wg[:, kc, cg * 512:(cg + 1) * 512],
                                  start=(kc == 0), stop=(kc == 7))
                        for kc in range(2):
                            fw.mm(pp, pT[:, kc, bb * 128:(bb + 1) * 128], wp[:, kc, cg * 512:(cg + 1) * 512],
                                  start=(kc == 0), stop=(kc == 1))
                        g = gs[pi % 2]
                        fw.act(g, pg, AF.Sigmoid)
                        fw.tt(g, g, pp, ALU.mult)
                        fw.tt(ht[:, bb, cg * 512:(cg + 1) * 512], ht[:, bb, cg * 512:(cg + 1) * 512], g, ALU.add)
                fw.dma('pool', Hout[t0:t0 + 512, :].rearrange("(b p) d -> p b d", p=128), ht)
                it += 1

    if 'C' in feat:
        for li, L in enumerate(Ls):
            gen_tables(li, L)
    Hb = [dt_scr("H2", [T, D]), dt_scr("H3", [T, D])]
    cur = xtok
    for l in range(depth):
        hi = 0
        def nxt():
            return [H[0], H[1], Hb[0], Hb[1]][[i for i in range(4) if [H[0], H[1], Hb[0], Hb[1]][i] is not cur][0]]
        need_rev = ('B' in feat) or ('D' in feat)
        norm_phase(cur, W['norm_mix'][l], XN, XNR if need_rev else None, "nm%d" % l)
        bids = []
        if 'A' in feat:
            mixer_a(l)
            bids.append(0)
        if 'B' in feat:
            mixer_gla(l, 'B')
            bids.append(1)
        if 'C' in feat:
            mixer_c(l)
            bids.append(2)
        if 'D' in feat:
            mixer_gla(l, 'D')
            bids.append(3)
        h1 = nxt()
        merge_phase(l, cur, h1, bids)
        cur = h1
        if 'ffn' in feat:
            norm_phase(cur, W['norm_ffn'][l], XN, None, "nf%d" % l)
            h2 = nxt()
            ffn_phase(l, cur, h2)
            cur = h2
        if 'ple' in feat:
            norm_phase(cur, W['norm_ple'][l], XN, None, "np%d" % l)
            h3 = nxt()
            ple_phase(l, cur, h3)
            cur = h3
    final_phase(cur)
    fw.barrier()
    cst.es.close()
    fw.es.close()
    return nc, fw


def make_consts(seqs):
    Ls = sorted(set(seqs))
    out = _make_consts0()
    j = np.arange(128, dtype=np.float32)
    out['c_r0'] = np.tile((2 * j + 1)[None, :], (128, 1)).astype(np.float32)
    out['c_t0'] = np.tile(np.arange(512, dtype=np.float32)[None, :], (128, 1))
    out['c_rate'] = np.abs(np.linspace(math.log(1e-2) / 0.3, math.log(1e-2) / 1.5, 512, dtype=np.float32)).astype(np.float32)
    for i, L in enumerate(Ls):
        t = np.linspace(0.0, 1.0, L, dtype=np.float32)[:, None]
        omega = (np.float32(2.0 * math.pi / L) * np.arange(L, dtype=np.float32))[:, None]
        bands = np.linspace(1e-4, 15, 16, dtype=np.float32)[None, :]
        feat = np.concatenate([t, np.cos(omega * bands), -np.sin(omega * bands)], axis=-1).astype(np.float32)
        out['c_feat%d' % i] = np.ascontiguousarray(feat.T)
        out['c_nt%d' % i] = np.ascontiguousarray(-t[:, 0])
        p = np.arange(128)[:, None]
        n = np.arange(L // 128)[None, :] * 128 + p
        out['c_fx%d' % i] = np.stack([n // 64, n % 64]).astype(np.float32)
        x = 2 * n + 1
        out['c_ix%d' % i] = np.stack([x // 64, x % 64]).astype(np.float32)
    return out


def _make_consts0():
    ident = np.eye(128, dtype=np.float32)
    jrev = np.ascontiguousarray(ident[::-1])
    s = np.arange(128)
    mask = ((s[:, None] <= s[None, :]) & ((s[:, None] // 32) == (s[None, :] // 32))).astype(np.float32)
    rst = np.ones((128, 512), np.float32)
    rst[:, ::32] = 0.0
    cm = np.zeros((128, 4, 128), np.float32)
    rm = np.zeros((128, 4), np.float32)
    for c in range(4):
        cm[:, c, 32 * c:32 * c + 32] = 1.0
        rm[32 * c:32 * c + 32, c] = 1.0
    rst128 = np.ones((128, 512), np.float32)
    rst128[:, ::128] = 0.0
    mask128 = (s[:, None] <= s[None, :]).astype(np.float32)
    return dict(c_ident=ident, c_jrev=jrev, c_mask=mask, c_rst=rst, c_cmask=cm.reshape(128, 512), c_rmask=rm,
                c_rst128=rst128, c_mask128=mask128,
                c_sel=np.repeat(np.eye(8, dtype=np.float32), 128, axis=1))


WNAMES = ['norm_mix', 'w_in', 'b_in', 'conv_a_w', 'conv_a_b', 'rglru_w', 'rglru_b', 'rglru_lam', 'mlstm_norm',
          'hyena_conv_w', 'hyena_conv_b', 'hyena_w1', 'hyena_b1', 'hyena_freq1', 'hyena_w2', 'hyena_b2',
          'hyena_freq2', 'hyena_w3', 'hyena_skip', 'hgrn_lb_logits', 'hgrn_norm', 'w_branch', 'w_out', 'norm_ffn',
          'w_up', 'ffn_conv_w', 'ffn_conv_b', 'w_down', 'norm_ple', 'w_ple_gate', 'w_ple', 'final_norm']

_CACHE = {}


def run(inputs, seqs_per_core, depth, groups, dbg=(), feat=('A', 'B', 'C', 'D', 'ffn', 'ple')):
    key = (tuple(seqs_per_core), depth, tuple(dbg), tuple(feat))
    if key not in _CACHE:
        _CACHE[key] = build(list(seqs_per_core), depth, dbg, feat)
    nc, fw = _CACHE[key]
    consts = make_consts(seqs_per_core)
    in_maps = []
    for g in groups:
        xs = np.concatenate([np.asarray(inputs['x_' + n][b], np.float32) for (n, b) in g], axis=0)
        ps = np.concatenate([np.asarray(inputs['p_' + n][:depth, b], np.float32) for (n, b) in g], axis=1)
        m = {"xtok": np.ascontiguousarray(xs), "ptok": np.ascontiguousarray(ps)}
        for k in WNAMES:
            a = np.asarray(inputs[k], np.float32)
            m[k] = a if k == 'final_norm' or k == 'hgrn_lb_logits' else a[:depth]
        m.update(consts)
        in_maps.append(m)
    if os.environ.get('K_TRACE'):
        res = run_bass_kernel_spmd(nc, in_maps, core_ids=list(range(len(groups))), trace=True)
        print("EXEC_NS", res.exec_time_ns)
    else:
        res = run_bass_kernel_spmd(nc, in_maps, core_ids=list(range(len(groups))))
    return res.results


def kernel(**inputs):
    groups = [[('prompt', 2 * c), ('prompt', 2 * c + 1), ('sample', c // 4)] for c in range(8)]
    res = run(inputs, (4096, 4096, 8192), 4, groups)
    yp = np.zeros((16, 4096, D), np.float32)
    ys = np.zeros((2, 8192, D), np.float32)
    for c in range(8):
        y = res[c]["ytok"]
        yp[2 * c] = y[0:4096]
        yp[2 * c + 1] = y[4096:8192]
        if c % 4 == 0:
            ys[c // 4] = y[8192:16384]
    return (yp, ys)
```
